# Optimizing a Trainium2 kernel written in Bass

```python
import jax, jax.numpy as jnp
from jax import lax
import numpy as np

D_MODEL = 2048
BATCH = 16
SEQ = 2048
DEPTH = 1

MIX_WIDTH = D_MODEL
GMLP_WIDTH = MIX_WIDTH // 2
GMLP_GROUP_DIM = 128
GMLP_GROUPS = GMLP_WIDTH // GMLP_GROUP_DIM
GMLP_CHUNK = 128
V_HEAD_DIM = 128
MLA_HEADS = (MIX_WIDTH - GMLP_WIDTH) // V_HEAD_DIM
QK_NOPE_DIM = 128
QK_ROPE_DIM = 64
QK_HEAD_DIM = QK_NOPE_DIM + QK_ROPE_DIM
Q_LORA_RANK = 512
KV_LORA_RANK = 512
ROPE_THETA = 10000.0
ATTN_BLOCK = 128
IN_COLS = 2 * GMLP_WIDTH + Q_LORA_RANK + KV_LORA_RANK + QK_ROPE_DIM
D_FF = 5504
EPS = 1e-6

kernel_name = "hymba_macaron_gmlp_mla_layer"


def rmsnorm(x, g):
    x32 = x.astype(jnp.float32)
    y = x32 * lax.rsqrt(jnp.mean(x32 * x32, axis=-1, keepdims=True) + EPS)
    return (y * g.astype(jnp.float32)).astype(x.dtype)


def swiglu(x, w_gate, w_up, w_down):
    return (jax.nn.silu(x @ w_gate) * (x @ w_up)) @ w_down


def rope_tables(positions):
    half = QK_ROPE_DIM // 2
    inv_freq = 1.0 / (ROPE_THETA ** (jnp.arange(half, dtype=jnp.float32) / half))
    ang = positions.astype(jnp.float32)[..., None] * inv_freq
    return jnp.cos(ang)[:, :, None, :], jnp.sin(ang)[:, :, None, :]


def apply_rope(x, cos, sin):
    x1, x2 = jnp.split(x.astype(jnp.float32), 2, axis=-1)
    return jnp.concatenate([x1 * cos - x2 * sin, x2 * cos + x1 * sin], axis=-1).astype(x.dtype)


def gmlp_mixer(z, v_norm_g, w_s, b_s):
    B, S, _ = z.shape
    z = jax.nn.gelu(z, approximate=False)
    u, v = jnp.split(z, 2, axis=-1)
    v = rmsnorm(v, v_norm_g)
    n_chunks = S // GMLP_CHUNK
    v = v.reshape(B, n_chunks, GMLP_CHUNK, GMLP_GROUPS, GMLP_GROUP_DIM)
    w_causal = jnp.tril(w_s)
    mixed = jnp.einsum('gts,bcsgd->bctgd', w_causal, v) + b_s.T[None, None, :, :, None]
    return u * mixed.reshape(B, S, GMLP_WIDTH)


def mla_mixer(c_q, c_kv, k_rope, cos, sin, q_norm_g, w_q_up, kv_norm_g, w_kv_up,
              q_head_g, k_head_g):
    B, S, _ = c_q.shape
    q = (rmsnorm(c_q, q_norm_g) @ w_q_up).reshape(B, S, MLA_HEADS, QK_HEAD_DIM)
    kv = (rmsnorm(c_kv, kv_norm_g) @ w_kv_up).reshape(B, S, MLA_HEADS, QK_NOPE_DIM + V_HEAD_DIM)
    k_nope, v = jnp.split(kv, [QK_NOPE_DIM], axis=-1)
    k_r = jnp.broadcast_to(k_rope[:, :, None, :], (B, S, MLA_HEADS, QK_ROPE_DIM))
    k = jnp.concatenate([k_nope, k_r], axis=-1)
    q = rmsnorm(q, q_head_g)
    k = rmsnorm(k, k_head_g)
    q = jnp.concatenate([q[..., :QK_NOPE_DIM], apply_rope(q[..., QK_NOPE_DIM:], cos, sin)], axis=-1)
    k = jnp.concatenate([k[..., :QK_NOPE_DIM], apply_rope(k[..., QK_NOPE_DIM:], cos, sin)], axis=-1)
    scale = QK_HEAD_DIM ** -0.5
    outs = []
    for i in range(S // ATTN_BLOCK):
        q_blk = q[:, i * ATTN_BLOCK:(i + 1) * ATTN_BLOCK]
        kv_len = (i + 1) * ATTN_BLOCK
        s = jnp.einsum('bqhd,bkhd->bhqk', q_blk, k[:, :kv_len]).astype(jnp.float32) * scale
        q_pos = i * ATTN_BLOCK + jnp.arange(ATTN_BLOCK)
        mask = q_pos[:, None] >= jnp.arange(kv_len)[None, :]
        s = jnp.where(mask[None, None], s, jnp.float32(-1e30))
        p = jax.nn.softmax(s, axis=-1).astype(v.dtype)
        outs.append(jnp.einsum('bhqk,bkhd->bqhd', p, v[:, :kv_len]))
    return jnp.concatenate(outs, axis=1)


def setup_inputs(seed: int = 0) -> dict:
    key = jax.random.key(seed)
    ks = jax.random.split(key, 32)
    f32 = jnp.float32

    def normal(k, shape, scale):
        return jax.random.normal(k, shape, f32) * scale

    def gain(k, shape):
        return 1.0 + 0.05 * jax.random.normal(k, shape, f32)

    L = DEPTH
    x = jax.random.normal(ks[0], (BATCH, SEQ, D_MODEL), f32)
    offsets = jax.random.randint(ks[1], (BATCH, 1), 0, 4096, dtype=jnp.int32)
    positions = (offsets + jnp.arange(SEQ, dtype=jnp.int32)[None, :]).astype(jnp.int32)
    return {
        "x": x,
        "positions": positions,
        "ffn1_norm_g": gain(ks[2], (L, D_MODEL)),
        "ffn1_w_gate": normal(ks[3], (L, D_MODEL, D_FF), D_MODEL ** -0.5),
        "ffn1_w_up": normal(ks[4], (L, D_MODEL, D_FF), D_MODEL ** -0.5),
        "ffn1_w_down": normal(ks[5], (L, D_FF, D_MODEL), D_FF ** -0.5),
        "mix_norm_g": gain(ks[6], (L, D_MODEL)),
        "w_in": normal(ks[7], (L, D_MODEL, IN_COLS), D_MODEL ** -0.5),
        "gmlp_v_norm_g": gain(ks[8], (L, GMLP_WIDTH)),
        "gmlp_w_s": normal(ks[9], (L, GMLP_GROUPS, GMLP_CHUNK, GMLP_CHUNK), 0.5 * GMLP_CHUNK ** -0.5),
        "gmlp_b_s": 1.0 + 0.1 * jax.random.normal(ks[10], (L, GMLP_GROUPS, GMLP_CHUNK), f32),
        "mla_q_norm_g": gain(ks[11], (L, Q_LORA_RANK)),
        "mla_w_q_up": normal(ks[12], (L, Q_LORA_RANK, MLA_HEADS * QK_HEAD_DIM), Q_LORA_RANK ** -0.5),
        "mla_kv_norm_g": gain(ks[13], (L, KV_LORA_RANK)),
        "mla_w_kv_up": normal(ks[14], (L, KV_LORA_RANK, MLA_HEADS * (QK_NOPE_DIM + V_HEAD_DIM)), KV_LORA_RANK ** -0.5),
        "mla_q_head_g": gain(ks[15], (L, QK_HEAD_DIM)),
        "mla_k_head_g": gain(ks[16], (L, QK_HEAD_DIM)),
        "gmlp_out_g": gain(ks[17], (L, GMLP_GROUPS, GMLP_GROUP_DIM)),
        "mla_out_g": gain(ks[18], (L, MLA_HEADS, V_HEAD_DIM)),
        "w_out": normal(ks[19], (L, MIX_WIDTH, D_MODEL), MIX_WIDTH ** -0.5),
        "ffn2_norm_g": gain(ks[20], (L, D_MODEL)),
        "ffn2_w_gate": normal(ks[21], (L, D_MODEL, D_FF), D_MODEL ** -0.5),
        "ffn2_w_up": normal(ks[22], (L, D_MODEL, D_FF), D_MODEL ** -0.5),
        "ffn2_w_down": normal(ks[23], (L, D_FF, D_MODEL), D_FF ** -0.5),
    }


def reference(x, positions, ffn1_norm_g, ffn1_w_gate, ffn1_w_up, ffn1_w_down, mix_norm_g, w_in,
              gmlp_v_norm_g, gmlp_w_s, gmlp_b_s, mla_q_norm_g, mla_w_q_up, mla_kv_norm_g,
              mla_w_kv_up, mla_q_head_g, mla_k_head_g, gmlp_out_g, mla_out_g, w_out,
              ffn2_norm_g, ffn2_w_gate, ffn2_w_up, ffn2_w_down):
    B, S, _ = x.shape
    cos, sin = rope_tables(positions)
    split_pts = [2 * GMLP_WIDTH, 2 * GMLP_WIDTH + Q_LORA_RANK,
                 2 * GMLP_WIDTH + Q_LORA_RANK + KV_LORA_RANK]
    for l in range(DEPTH):
        x = x + 0.5 * swiglu(rmsnorm(x, ffn1_norm_g[l]), ffn1_w_gate[l], ffn1_w_up[l], ffn1_w_down[l])
        h = rmsnorm(x, mix_norm_g[l])
        z = h @ w_in[l]
        z_a, c_q, c_kv, k_rope = jnp.split(z, split_pts, axis=-1)
        y_a = gmlp_mixer(z_a, gmlp_v_norm_g[l], gmlp_w_s[l], gmlp_b_s[l])
        y_a = rmsnorm(y_a.reshape(B, S, GMLP_GROUPS, GMLP_GROUP_DIM), gmlp_out_g[l])
        y_b = mla_mixer(c_q, c_kv, k_rope, cos, sin, mla_q_norm_g[l], mla_w_q_up[l],
                        mla_kv_norm_g[l], mla_w_kv_up[l], mla_q_head_g[l], mla_k_head_g[l])
        y_b = rmsnorm(y_b, mla_out_g[l])
        y = jnp.concatenate([y_a.reshape(B, S, GMLP_WIDTH), y_b.reshape(B, S, MLA_HEADS * V_HEAD_DIM)], axis=-1)
        x = x + y @ w_out[l]
        x = x + 0.5 * swiglu(rmsnorm(x, ffn2_norm_g[l]), ffn2_w_gate[l], ffn2_w_up[l], ffn2_w_down[l])
    return x
```

```python
import contextlib
import numpy as np
import concourse.bass as bass
import concourse.mybir as mybir
from concourse.bass_utils import run_bass_kernel_spmd

F32 = mybir.dt.float32
BF16 = mybir.dt.bfloat16
I32 = mybir.dt.int32
AF = mybir.ActivationFunctionType
ALU = mybir.AluOpType

SEM_CAP = 12000

D = 2048
DC = 16
FF = 5504
NFC = 43
TT = 512
QUARTERS = [(0, 11), (11, 11), (22, 11), (33, 10)]
EPS = 1e-6
NCORES = 8
TILES_PER_CORE = 8
NXS = 20
NCOL = 32
C_GQ, C_GKV, C_GQN, C_GQR, C_GQS, C_GKN, C_GKR, C_GKS, C_GOA, C_GOB, C_IF, C_SG = 0, 4, 8, 9, 10, 11, 12, 13, 14, 22, 30, 31
PI = 3.141592653589793


class Chan:
    def __init__(self, name, step):
        self.name = name
        self.step = step
        self.count = 0
        self.needed = set()
        self.sems = []
        self.pos = None


class Buf:
    def __init__(self, name, ap, excl=False):
        self.name = name
        self.ap = ap
        self.last_write = None
        self.readers = {}
        self.chan = None
        self.excl = excl


class Op:
    __slots__ = ("fn", "chan", "idx", "deps")

    def __init__(self, fn, chan, idx, deps):
        self.fn = fn
        self.chan = chan
        self.idx = idx
        self.deps = deps


class Prog:
    ENGINES = ("pe", "act", "dve", "pool", "sp")

    def __init__(self, nc, dry=False):
        self.nc = nc
        self.dry = dry
        self.ops = {e: [] for e in self.ENGINES}
        self.echan = {e: Chan("e_" + e, 1) for e in self.ENGINES}
        self.seen = {e: {} for e in self.ENGINES}
        self.chans = list(self.echan.values())
        self.stack = contextlib.ExitStack()

    def sbuf(self, name, shape, dtype):
        return self.stack.enter_context(self.nc.sbuf_tensor(name, shape, dtype))

    def psum(self, name, shape, dtype):
        return self.stack.enter_context(self.nc.psum_tensor(name, shape, dtype))

    def dma_chan(self, t):
        if t.chan is None:
            t.chan = Chan("d_" + t.name, 16)
            self.chans.append(t.chan)
        return t.chan

    def _add(self, engine, fn, reads, writes, chan):
        if self.dry:
            return None
        idx = chan.count
        chan.count += 1
        if any(t.excl for t in reads):
            writes = list(writes) + [t for t in reads if t.excl and t not in writes]
            reads = [t for t in reads if not t.excl]
        deps = []
        for t in reads:
            if t.last_write is not None:
                deps.append(t.last_write)
        for t in writes:
            if t.last_write is not None:
                deps.append(t.last_write)
            deps.extend(t.readers.items())
        seen = self.seen[engine]
        own = self.echan[engine]
        final = {}
        for (c, i) in deps:
            if c is own and engine == "pe":
                continue
            if seen.get(c, -1) >= i:
                continue
            if final.get(c, -1) < i:
                final[c] = i
        for c, i in final.items():
            seen[c] = i
            c.needed.add(i)
        op = Op(fn, chan, idx, list(final.items()))
        self.ops[engine].append(op)
        for t in reads:
            if t.readers.get(chan, -1) < idx:
                t.readers[chan] = idx
        for t in writes:
            t.last_write = (chan, idx)
            t.readers = {}
        return op

    def op(self, engine, fn, reads=(), writes=()):
        return self._add(engine, fn, reads, writes, self.echan[engine])

    def dma(self, engine, out_ap, in_ap, reads=(), writes=(), chan_buf=None):
        if self.dry:
            return None
        if chan_buf is None:
            chan_buf = writes[0] if writes else reads[0]
        chan = self.dma_chan(chan_buf)
        chan.needed.add(chan.count)

        def fn(eng):
            return eng.dma_start(out=out_ap, in_=in_ap)

        return self._add(engine, fn, reads, writes, chan)

    def wait_bufs(self, engine, bufs):
        return self._add(engine, lambda eng: None, [], list(bufs), self.echan[engine])

    def emit(self):
        nc = self.nc
        st = self.stack
        for c in self.chans:
            needed = sorted(c.needed)
            c.pos = {i: p for p, i in enumerate(needed)}
            nsem = (len(needed) + SEM_CAP - 1) // SEM_CAP
            c.sems = [st.enter_context(nc.semaphore(f"{c.name}_{k}")) for k in range(nsem)]
        self.nsems = sum(len(c.sems) for c in self.chans)

        def semval(c, i):
            p = c.pos[i]
            return c.sems[p // SEM_CAP], (p % SEM_CAP + 1) * c.step

        def run(engine, eng):
            for op in self.ops[engine]:
                for (c, i) in op.deps:
                    s, v = semval(c, i)
                    eng.wait_ge(s, v)
                ins = op.fn(eng)
                if ins is not None and op.idx in op.chan.needed:
                    s, v = semval(op.chan, op.idx)
                    ins.then_inc(s, op.chan.step)

        block = st.enter_context(nc.Block())

        @block.tensor
        def _(eng):
            run("pe", eng)

        @block.scalar
        def _(eng):
            run("act", eng)

        @block.vector
        def _(eng):
            run("dve", eng)

        @block.gpsimd
        def _(eng):
            run("pool", eng)

        @block.sync
        def _(eng):
            run("sp", eng)

    def close(self):
        self.stack.close()


class Ring:
    def __init__(self, P, name, nslots, slot_elems):
        self.P = P
        self.name = name
        self.n = nslots
        self.slots = []
        for i in range(nslots):
            t = P.sbuf(f"{name}{i}", [128, slot_elems], BF16)
            self.slots.append((t, Buf(f"{name}{i}", t[:])))
        self.reqs = []
        self.n_acq = 0
        self.n_loaded = 0

    def view(self, i, shape):
        t = self.slots[i % self.n][0]
        n = 1
        for s in shape[1:]:
            n *= s
        ap = t[0:shape[0], 0:n]
        if len(shape) == 3:
            ap = ap.rearrange("p (a b) -> p a b", a=shape[1])
        return ap

    def _load(self, i):
        src, shape = self.reqs[i]
        buf = self.slots[i % self.n][1]
        self.P.dma("pool", self.view(i, shape), src, writes=[buf])

    def acquire(self, src_ap, shape):
        P = self.P
        i = self.n_acq
        self.n_acq += 1
        if P.dry:
            self.reqs.append((src_ap, tuple(shape)))
            return None, None
        assert self.reqs[i][1] == tuple(shape)
        while self.n_loaded <= i:
            self._load(self.n_loaded)
            self.n_loaded += 1
        return self.view(i, shape), self.slots[i % self.n][1]

    def acquire_idx(self, src_ap, shape):
        i = self.n_acq
        v, b = self.acquire(src_ap, shape)
        return v, b, i

    def prefetch(self):
        if self.P.dry:
            return
        while self.n_loaded < min(self.n, len(self.reqs)):
            self._load(self.n_loaded)
            self.n_loaded += 1

    def release(self, i=None):
        if self.P.dry:
            return
        if i is None:
            i = self.n_acq - 1
        nxt = i + self.n
        if nxt < len(self.reqs) and self.n_loaded == nxt:
            self._load(nxt)
            self.n_loaded += 1


class K:
    pass


def setup(nc, P, n_tiles, stages):
    k = K()
    k.nc, k.P, k.n_tiles, k.stages = nc, P, n_tiles, stages
    dt = lambda name, shape, dtype=F32, kind="ExternalInput": nc.dram_tensor(name, shape, dtype, kind=kind).ap()
    k.x_d = dt("xT", [n_tiles, 128, DC, TT])
    k.o_d = dt("oT", [n_tiles, 128, DC, TT], kind="ExternalOutput")
    if "ffn1" in stages:
        k.w1g = dt("w1g", [NFC, 128, DC, 128])
        k.w1u = dt("w1u", [NFC, 128, DC, 128])
        k.w1d = dt("w1d", [4, 128, NFC, 512])
    if "ffn2" in stages:
        k.w2g = dt("w2g", [NFC, 128, DC, 128])
        k.w2u = dt("w2u", [NFC, 128, DC, 128])
        k.w2d = dt("w2d", [4, 128, NFC, 512])
    k.gains_d = dt("gains", [128, 3 * DC])

    xt = P.sbuf("xt", [128, NXS, TT], F32)
    k.xslots = [Buf(f"xt{c}", xt[:, c, :]) for c in range(NXS)]
    k.xt_t = xt
    for c_, b_ in enumerate(k.xslots):
        b_.slot = c_
    k.xt = k.xslots[0:DC]
    ht = P.sbuf("ht", [128, DC, TT], BF16)
    k.ht = [Buf(f"ht{c}", ht[:, c, :]) for c in range(DC)]
    at = P.sbuf("at", [128, 22, TT], BF16)
    k.at = [Buf(f"at{c}", at[:, c, :]) for c in range(22)]
    st = P.sbuf("tmp2", [128, 4 * TT], F32)
    k.tmp2_t = st
    k.tmp2 = [Buf(f"tmp2_{c}", st[:, c * TT:(c + 1) * TT]) for c in range(4)]
    k.silu = k.tmp2[0:2]
    k.rstd = k.tmp2[2]
    gains = P.sbuf("gains_sb", [128, 3 * DC], F32)
    k.gains = Buf("gains", gains[:])
    k.gains_t = gains
    eps_t = P.sbuf("eps", [128, 1], F32)
    k.eps = Buf("eps", eps_t[:])
    k.eps_t = eps_t
    ones = P.sbuf("ones", [128, 128], BF16)
    k.ones = Buf("ones", ones[:])
    k.ones_t = ones
    k.w4 = Ring(P, "w4_", 6, 2048)
    k.wd = Ring(P, "wd_", 3, 11 * 512)
    psall = P.psum("psall", [128, 8 * 512], F32)
    k.ps_all = psall
    k.ps = [Buf(f"ps{i}", psall[:, i * 512:(i + 1) * 512], excl=True) for i in range(8)]
    k.gu_ctr = 0
    k.dg_ctr = 0
    k.ht_t, k.at_t = ht, at
    if "mix" in stages:
        k.win = dt("win", [26, 128, DC, 128])
        k.wq = dt("wq", [8, 128, 4, 256])
        k.wkv = dt("wkv", [8, 128, 4, 256])
        k.wo = dt("wo", [DC, 128, DC, 128])
        k.cols_d = dt("cols", [128, NCOL])
        k.gvbc_d = dt("gvbc", [1, 1024])
        k.bsbc_d = dt("bsbc", [1, 1024])
        k.wst_d = dt("wst", [128, 1024])
        k.pos_d = dt("pos", [n_tiles, 1, TT], I32)
        cols = P.sbuf("cols_sb", [128, NCOL], F32)
        k.cols_t, k.cols = cols, Buf("cols", cols[:])
        gvbc = P.sbuf("gvbc_sb", [128, 1024], F32)
        k.gvbc_t, k.gvbc = gvbc, Buf("gvbc", gvbc[:])
        bsbc = P.sbuf("bsbc_sb", [128, 1024], F32)
        k.bsbc_t, k.bsbc = bsbc, Buf("bsbc", bsbc[:])
        wct = P.sbuf("wct", [128, 1024], BF16)
        k.wct_t, k.wct = wct, Buf("wct", wct[:])
        mask = P.sbuf("mask01", [128, 128], BF16)
        k.mask_t, k.mask = mask, Buf("mask", mask[:])
        tmp = P.sbuf("tmp", [128, 4 * TT], F32)
        k.tmp_t = tmp
        k.tmp = [Buf(f"tmp{j}", tmp[:, j * TT:(j + 1) * TT]) for j in range(4)]
        yt = P.sbuf("yt", [128, DC, TT], BF16)
        k.yt_t = yt
        k.yt = [Buf(f"yt{c}", yt[:, c, :]) for c in range(DC)]
        ckv = P.sbuf("ckv", [128, 4, 2048], BF16)
        k.ckv_t = ckv
        k.ckv = [[Buf(f"ckv{kc}_{j}", ckv[:, kc, j * TT:(j + 1) * TT]) for j in range(4)] for kc in range(4)]
        krb = P.sbuf("krb", [128, 2048], BF16)
        k.krb_t = krb
        k.krb = [Buf(f"krb{j}", krb[:, j * TT:(j + 1) * TT]) for j in range(4)]
        krsq = P.sbuf("krsq", [64, TT], BF16)
        k.krsq = Buf("krsq", krsq[:])
        sck = P.sbuf("sck", [128, 128], F32)
        k.sck_t = sck
        k.sck = [Buf(f"sck{j}", sck[:, j * 32:(j + 1) * 32]) for j in range(4)]
        ssk = P.sbuf("ssk", [128, 32], F32)
        k.ssk_t, k.ssk = ssk, Buf("ssk", ssk[:])
        ssr = P.sbuf("ssr", [128, 4], F32)
        k.ssr_t, k.ssr = ssr, Buf("ssr", ssr[:])
        e192 = P.sbuf("e192", [128, 1], F32)
        k.e192_t, k.e192 = e192, Buf("e192", e192[:])
        k.wkall = dt("wkall", [2, 128, 4, 512])
        cs = P.sbuf("cos_sb", [64, TT], F32)
        k.cos = Buf("cos", cs[:])
        sn = P.sbuf("sin_sb", [64, TT], F32)
        k.sin = Buf("sin", sn[:])
        posi = P.sbuf("posi", [64, TT], I32)
        k.posi = Buf("posi", posi[:])
        ki = P.sbuf("ki", [64, TT], I32)
        k.ki = Buf("ki", ki[:])
        ssc = P.sbuf("ssc", [128, 8], F32)
        k.ssc_t, k.ssc = ssc, Buf("ssc", ssc[:])
        k.sscb = [Buf(f"ssc{j}", ssc[:, j:j + 1]) for j in range(4)]
    return k


def prologue(k):
    P = k.P
    P.dma("sp", k.gains.ap, k.gains_d, writes=[k.gains])
    P.op("dve", lambda e: e.memset(k.ones.ap, 1.0), writes=[k.ones])
    P.op("dve", lambda e: e.memset(k.eps.ap, EPS), writes=[k.eps])
    if "mix" in k.stages:
        P.op("dve", lambda e: e.memset(k.e192.ap, 192.0 * EPS), writes=[k.e192])
        P.op("dve", lambda e: e.memset(k.krb_t[64:128, :], 0.0), writes=list(k.krb))
        P.dma("sp", k.cols.ap, k.cols_d, writes=[k.cols])
        P.dma("sp", k.gvbc.ap, k.gvbc_d.partition_broadcast(128), writes=[k.gvbc])
        P.dma("sp", k.bsbc.ap, k.bsbc_d.partition_broadcast(128), writes=[k.bsbc])
        wsf = k.tmp_t[:, 0:1024]
        P.dma("sp", wsf, k.wst_d, writes=[k.tmp[0], k.tmp[1]])
        pat = [[0, 8], [1, 128]]
        P.op("pool", lambda e: e.affine_select(wsf.rearrange("p (a b) -> p a b", a=8), wsf.rearrange("p (a b) -> p a b", a=8),
                                               pat, ALU.is_ge, 0.0, base=0, channel_multiplier=-1),
             reads=[k.tmp[0], k.tmp[1]], writes=[k.tmp[0], k.tmp[1]])
        P.op("dve", lambda e: e.tensor_copy(k.wct.ap, wsf), reads=[k.tmp[0], k.tmp[1]], writes=[k.wct])
        P.op("pool", lambda e: e.affine_select(k.mask.ap, k.ones.ap, [[1, 128]], ALU.is_ge, 0.0, base=0, channel_multiplier=-1),
             reads=[k.ones], writes=[k.mask])


def mm(P, out, lhsT_ap, lhsT_buf, rhs_ap, rhs_buf, start, stop, out_ap=None):
    oap = out.ap if out_ap is None else out_ap
    P.op("pe", lambda e: e.matmul(oap, lhsT_ap, rhs_ap, start=start, stop=stop),
         reads=[lhsT_buf, rhs_buf], writes=[out])


def rms_rstd(k, srcs, sq_bufs, n_feat, ps, rstd, rows=None):
    P = k.P
    n = len(srcs)
    for i in range(n):
        sb, sap, nr = srcs[i][:3]
        if len(srcs[i]) > 3 and srcs[i][3]:
            mm(P, ps, k.ones_t[0:nr, :], k.ones, sap, sb, i == 0, i == n - 1)
            continue
        q = sq_bufs[i]
        qap = q.ap[0:nr, :]
        P.op("act", lambda e, sap=sap, qap=qap: e.activation(qap, sap, AF.Square), reads=[sb], writes=[q])
        mm(P, ps, k.ones_t[0:nr, :], k.ones, qap, q, i == 0, i == n - 1)
    P.op("act", lambda e: e.activation(rstd.ap, ps.ap, AF.Ln, bias=k.eps_t[:, 0:1], scale=1.0 / n_feat),
         reads=[ps, k.eps], writes=[rstd])
    P.op("act", lambda e: e.activation(rstd.ap, rstd.ap, AF.Exp, scale=-0.5), reads=[rstd], writes=[rstd])


def norm_to_ht(k, gidx):
    P = k.P
    gt = k.gains_t
    rms_rstd(k, [(b, b.ap, 128) for b in k.xt], k.at[:DC], D, k.ps[4], k.rstd)
    for c in range(DC):
        x, h = k.xt[c], k.ht[c]
        gap = gt[:, gidx * DC + c: gidx * DC + c + 1]
        P.op("dve", lambda e, x=x, h=h, gap=gap: e.scalar_tensor_tensor(
            out=h.ap, in0=x.ap, scalar=gap, in1=k.rstd.ap, op0=ALU.mult, op1=ALU.mult),
            reads=[x, k.rstd, k.gains], writes=[h])


def ffn(k, gidx, wg, wu, wd, hook=None):
    P = k.P
    norm_to_ht(k, gidx)

    def gu_phase(q):
        f0, nf = QUARTERS[q]
        s = (q % 2) * 11
        fstart = 0
        if q == 0:
            items = []
            for fi in (0, 1):
                pg = k.ps[(k.gu_ctr % 2) * 2]
                pu = k.ps[(k.gu_ctr % 2) * 2 + 1]
                sl = k.silu[k.gu_ctr % 2]
                k.gu_ctr += 1
                vg, bg, ig = k.w4.acquire_idx(wg[f0 + fi], (128, DC, 128))
                vu, bu, iu = k.w4.acquire_idx(wu[f0 + fi], (128, DC, 128))
                items.append((fi, pg, pu, sl, vg, bg, ig, vu, bu, iu))
            if not P.dry:
                for c in range(DC):
                    for (fi, pg, pu, sl, vg, bg, ig, vu, bu, iu) in items:
                        mm(P, pg, vg[:, c, :], bg, k.ht[c].ap, k.ht[c], c == 0, c == DC - 1)
                        mm(P, pu, vu[:, c, :], bu, k.ht[c].ap, k.ht[c], c == 0, c == DC - 1)
            for (fi, pg, pu, sl, vg, bg, ig, vu, bu, iu) in items:
                k.w4.release(ig)
                k.w4.release(iu)
            for (fi, pg, pu, sl, vg, bg, ig, vu, bu, iu) in items:
                a = k.at[s + fi]
                P.op("act", lambda e, sl=sl, pg=pg: e.activation(sl.ap, pg.ap, AF.Silu), reads=[pg], writes=[sl])
                P.op("dve", lambda e, a=a, sl=sl, pu=pu: e.tensor_tensor(a.ap, sl.ap, pu.ap, ALU.mult),
                     reads=[sl, pu], writes=[a])
            fstart = 2
        for fi in range(fstart, nf):
            f = f0 + fi
            pg = k.ps[(k.gu_ctr % 2) * 2]
            pu = k.ps[(k.gu_ctr % 2) * 2 + 1]
            sl = k.silu[k.gu_ctr % 2]
            k.gu_ctr += 1
            for (w, pb) in ((wg, pg), (wu, pu)):
                v, b = k.w4.acquire(w[f], (128, DC, 128))
                if not P.dry:
                    for c in range(DC):
                        mm(P, pb, v[:, c, :], b, k.ht[c].ap, k.ht[c], c == 0, c == DC - 1)
                k.w4.release()
            a = k.at[s + fi]
            P.op("act", lambda e, sl=sl, pg=pg: e.activation(sl.ap, pg.ap, AF.Silu), reads=[pg], writes=[sl])
            P.op("dve", lambda e, a=a, sl=sl, pu=pu: e.tensor_tensor(a.ap, sl.ap, pu.ap, ALU.mult),
                 reads=[sl, pu], writes=[a])

    def down_phase(q):
        f0, nf = QUARTERS[q]
        s = (q % 2) * 11
        for g in range(4):
            v, b = k.wd.acquire(wd[g, :, f0:f0 + nf, :], (128, nf, 512))
            pbase = 4 if k.dg_ctr % 2 == 0 else 0
            k.dg_ctr += 1
            if not P.dry:
                for fi in range(nf):
                    a = k.at[s + fi]
                    for dd in range(4):
                        mm(P, k.ps[pbase + dd], v[:, fi, dd * 128:(dd + 1) * 128], b, a.ap, a, fi == 0, fi == nf - 1)
            k.wd.release()
            for dd in range(4):
                x = k.xt[g * 4 + dd]
                pb = k.ps[pbase + dd]
                P.op("dve", lambda e, x=x, pb=pb: e.scalar_tensor_tensor(
                    out=x.ap, in0=pb.ap, scalar=0.5, in1=x.ap, op0=ALU.mult, op1=ALU.add),
                    reads=[pb, x], writes=[x])
            if hook is not None and q == 3:
                hook(g)

    gu_phase(0)
    k.wd.prefetch()
    gu_phase(1)
    down_phase(0)
    gu_phase(2)
    down_phase(1)
    gu_phase(3)
    down_phase(2)
    down_phase(3)


def rope_tables(k, t):
    P = k.P
    T = k.tmp_t
    col = k.cols_t
    P.dma("sp", k.posi.ap, k.pos_d[t].partition_broadcast(64), writes=[k.posi])
    pf, ang, tt, a = (T[0:64, j * TT:(j + 1) * TT] for j in range(4))
    tb = k.tmp
    P.op("dve", lambda e: e.tensor_copy(pf, k.posi.ap), reads=[k.posi], writes=[tb[0]])
    P.op("dve", lambda e: e.tensor_scalar(ang, pf, col[0:64, C_IF:C_IF + 1], None, ALU.mult),
         reads=[tb[0], k.cols], writes=[tb[1]])
    for (dst, sh_t, sh_a, is_sin) in ((k.sin, 0.5, 0.0, True), (k.cos, 0.75, PI / 2, False)):
        P.op("dve", lambda e, sh_t=sh_t: e.tensor_scalar(tt, ang, 1.0 / (2 * PI), sh_t, ALU.mult, ALU.add),
             reads=[tb[1]], writes=[tb[2]])
        P.op("dve", lambda e: e.tensor_copy(k.ki.ap, tt), reads=[tb[2]], writes=[k.ki])
        P.op("dve", lambda e: e.tensor_copy(tt, k.ki.ap), reads=[k.ki], writes=[tb[2]])
        P.op("dve", lambda e: e.scalar_tensor_tensor(out=a, in0=tt, scalar=-2 * PI, in1=ang, op0=ALU.mult, op1=ALU.add),
             reads=[tb[2], tb[1]], writes=[tb[3]])
        if sh_a != 0.0:
            P.op("dve", lambda e, sh_a=sh_a: e.tensor_scalar(a, a, sh_a, None, ALU.add), reads=[tb[3]], writes=[tb[3]])
        P.op("dve", lambda e: e.tensor_scalar(tt, a, -PI, 2 * PI, ALU.is_lt, ALU.mult), reads=[tb[3]], writes=[tb[2]])
        P.op("dve", lambda e: e.tensor_tensor(a, a, tt, ALU.add), reads=[tb[3], tb[2]], writes=[tb[3]])
        P.op("dve", lambda e: e.tensor_scalar(a, a, PI, -PI, ALU.min, ALU.max), reads=[tb[3]], writes=[tb[3]])
        if is_sin:
            P.op("act", lambda e, dst=dst: e.activation(dst.ap, a, AF.Sin, scale=col[0:64, C_SG:C_SG + 1]),
                 reads=[tb[3], k.cols], writes=[dst])
        else:
            P.op("act", lambda e, dst=dst: e.activation(dst.ap, a, AF.Sin), reads=[tb[3]], writes=[dst])


def rope_apply(k, ps_x, ps_sw, c_g, c_gs, dst_ap, dst_bufs, rstd_ap=None, rstd_buf=None, tbufs=None):
    P = k.P
    col = k.cols_t
    tb = k.tmp if tbufs is None else tbufs
    t1, t2 = tb[0].ap[0:64, :], tb[1].ap[0:64, :]
    P.op("dve", lambda e: e.scalar_tensor_tensor(out=t1, in0=ps_x.ap[0:64, :], scalar=col[0:64, c_g:c_g + 1], in1=k.cos.ap,
                                                 op0=ALU.mult, op1=ALU.mult), reads=[ps_x, k.cols, k.cos], writes=[tb[0]])
    P.op("dve", lambda e: e.scalar_tensor_tensor(out=t2, in0=ps_sw.ap[0:64, :], scalar=col[0:64, c_gs:c_gs + 1], in1=k.sin.ap,
                                                 op0=ALU.mult, op1=ALU.mult), reads=[ps_sw, k.cols, k.sin], writes=[tb[1]])
    if rstd_ap is None:
        P.op("dve", lambda e: e.tensor_tensor(dst_ap, t1, t2, ALU.add), reads=[tb[0], tb[1]], writes=dst_bufs)
    else:
        P.op("dve", lambda e: e.tensor_tensor(t1, t1, t2, ALU.add), reads=[tb[0], tb[1]], writes=[tb[0]])
        P.op("dve", lambda e: e.tensor_tensor(dst_ap, t1, rstd_ap, ALU.mult), reads=[tb[0], rstd_buf], writes=dst_bufs)


def rope_pre(k, ps_x, ps_sw, c_g, c_gs, tbufs):
    P = k.P
    col = k.cols_t
    tb = tbufs
    t1, t2 = tb[0].ap[0:64, :], tb[1].ap[0:64, :]
    P.op("dve", lambda e: e.scalar_tensor_tensor(out=t1, in0=ps_x.ap[0:64, :], scalar=col[0:64, c_g:c_g + 1], in1=k.cos.ap,
                                                 op0=ALU.mult, op1=ALU.mult), reads=[ps_x, k.cols, k.cos], writes=[tb[0]])
    P.op("dve", lambda e: e.scalar_tensor_tensor(out=t2, in0=ps_sw.ap[0:64, :], scalar=col[0:64, c_gs:c_gs + 1], in1=k.sin.ap,
                                                 op0=ALU.mult, op1=ALU.mult), reads=[ps_sw, k.cols, k.sin], writes=[tb[1]])
    P.op("dve", lambda e: e.tensor_tensor(t1, t1, t2, ALU.add), reads=[tb[0], tb[1]], writes=[tb[0]])


def rope_post(k, dst_ap, dst_bufs, rstd_ap, rstd_buf, tbufs):
    P = k.P
    t1 = tbufs[0].ap[0:64, :]
    P.op("dve", lambda e: e.tensor_tensor(dst_ap, t1, rstd_ap, ALU.mult), reads=[tbufs[0], rstd_buf], writes=dst_bufs)


def mixer(k, t):
    P = k.P
    i = t % 4
    col = k.cols_t
    ps, at, ht, tmp, yt = k.ps, k.at, k.ht, k.tmp, k.yt
    PA, T = k.ps_all, k.tmp_t
    norm_to_ht(k, 1)
    rope_tables(k, t)
    uitems = []
    for c in range(4):
        v, b, iu_ = k.w4.acquire_idx(k.win[c], (128, DC, 128))
        uitems.append((c, v, b, iu_))
    if not P.dry:
        for kc in range(DC):
            for (c, v, b, iu_) in uitems:
                mm(P, ps[c], v[:, kc, :], b, ht[kc].ap, ht[kc], kc == 0, kc == DC - 1)
    for (c, v, b, iu_) in uitems:
        k.w4.release(iu_)
    for (c, v, b, iu_) in uitems:
        P.op("act", lambda e, c=c: e.activation(at[c].ap, ps[c].ap, AF.Gelu), reads=[ps[c]], writes=[at[c]])
    for c in range(4, 8):
        v, b = k.w4.acquire(k.win[c], (128, DC, 128))
        pb = ps[c % 4]
        if not P.dry:
            for kc in range(DC):
                mm(P, pb, v[:, kc, :], b, ht[kc].ap, ht[kc], kc == 0, kc == DC - 1)
        k.w4.release()
        P.op("act", lambda e, pb=pb, c=c: e.activation(at[c].ap, pb.ap, AF.Gelu), reads=[pb], writes=[at[c]])
    for (c0, cg, dsts) in ((16, C_GQ, [at[16 + j] for j in range(4)]), (20, C_GKV, [k.ckv[j][i] for j in range(4)])):
        for j in range(4):
            v, b = k.w4.acquire(k.win[c0 + j], (128, DC, 128))
            pb = ps[j]
            if not P.dry:
                for kc in range(DC):
                    mm(P, pb, v[:, kc, :], b, ht[kc].ap, ht[kc], kc == 0, kc == DC - 1)
            k.w4.release()
            P.op("act", lambda e, pb=pb, j=j: e.copy(tmp[j].ap, pb.ap), reads=[pb], writes=[tmp[j]])
        rms_rstd(k, [(tmp[j], tmp[j].ap, 128) for j in range(4)], [at[20], at[21], at[20], at[21]], 512, ps[4], k.rstd)
        for j in range(4):
            P.op("dve", lambda e, j=j, d=dsts[j], cg=cg: e.scalar_tensor_tensor(
                out=d.ap, in0=tmp[j].ap, scalar=col[:, cg + j:cg + j + 1], in1=k.rstd.ap, op0=ALU.mult, op1=ALU.mult),
                reads=[tmp[j], k.cols, k.rstd], writes=[dsts[j]])
    for (c, pb) in ((24, ps[4]), (25, ps[5])):
        v, b = k.w4.acquire(k.win[c], (128, DC, 128))
        if not P.dry:
            for kc in range(DC):
                mm(P, pb, v[:, kc, 0:64], b, ht[kc].ap, ht[kc], kc == 0, kc == DC - 1, out_ap=pb.ap[0:64, :])
        k.w4.release()
    P.op("act", lambda e: e.activation(k.krsq.ap, ps[4].ap[0:64, :], AF.Square), reads=[ps[4]], writes=[k.krsq])
    rope_apply(k, ps[4], ps[5], C_GKR, C_GKS, k.krb[i].ap[0:64, :], [k.krb[i]])
    for c in range(8):
        v, b = k.w4.acquire(k.win[8 + c], (128, DC, 128))
        if not P.dry:
            for sub in range(4):
                bank = ps[sub * 2 + c // 4]
                oap = PA[:, sub * 1024 + c * 128: sub * 1024 + (c + 1) * 128]
                for kc in range(DC):
                    mm(P, bank, k.ht_t[:, kc, sub * 128:(sub + 1) * 128], ht[kc], v[:, kc, :], b, kc == 0, kc == DC - 1, out_ap=oap)
        k.w4.release()
    def gm_stages(sub):
        if sub % 2 == 0:
            tA, tB = T[:, 0:1024], T[:, 1024:2048]
            tmp = k.tmp
            sq2 = k.at_t[:, 20:22, :].rearrange("p a b -> p (a b)")
            sqb = [at[20], at[21]]
        else:
            tA, tB = k.tmp2_t[:, 0:1024], k.tmp2_t[:, 1024:2048]
            tmp = k.tmp2
            sq2 = k.ht_t[:, 0:2, :].rearrange("p a b -> p (a b)")
            sqb = [ht[0], ht[1]]
        b01 = [ps[2 * sub], ps[2 * sub + 1]]
        psv = PA[:, sub * 1024:(sub + 1) * 1024]
        ssc = k.ssc_t[:, sub:sub + 1]
        rsc = k.ssc_t[:, 4 + sub:5 + sub]
        sscb = k.sscb[sub]
        vtok = k.at_t[:, 8 + 2 * sub:10 + 2 * sub, :].rearrange("p a b -> p (a b)")
        vb = [at[8 + 2 * sub], at[9 + 2 * sub]]
        uap = k.at_t[:, 0:8, sub * 128:(sub + 1) * 128]
        tA3, tB3 = tA.rearrange("p (a b) -> p a b", a=8), tB.rearrange("p (a b) -> p a b", a=8)
        t01, t23, t0123 = [tmp[0], tmp[1]], [tmp[2], tmp[3]], [tmp[0], tmp[1], tmp[2], tmp[3]]

        def s1():
            P.op("act", lambda e: e.activation(tA, psv, AF.Gelu), reads=b01, writes=t01)

        def s2():
            P.op("act", lambda e: e.activation(sq2, tA, AF.Square, accum_out=ssc), reads=t01, writes=sqb + [sscb])

        def s3():
            P.op("act", lambda e: e.activation(rsc, ssc, AF.Ln, bias=k.eps_t[:, 0:1], scale=1.0 / 1024),
                 reads=[sscb, k.eps], writes=[sscb])

        def s4():
            P.op("act", lambda e: e.activation(rsc, rsc, AF.Exp, scale=-0.5), reads=[sscb], writes=[sscb])

        def s5():
            P.op("dve", lambda e: e.scalar_tensor_tensor(out=vtok, in0=tA, scalar=rsc, in1=k.gvbc_t[:], op0=ALU.mult, op1=ALU.mult),
                 reads=t01 + [sscb, k.gvbc], writes=vb)

        def s6():
            for g in range(8):
                lhs = k.at_t[:, 8 + 2 * sub + g // 4, (g % 4) * 128:(g % 4 + 1) * 128]
                mm(P, b01[g // 4], lhs, vb[g // 4], k.wct_t[:, g * 128:(g + 1) * 128], k.wct, True, True,
                   out_ap=psv[:, g * 128:(g + 1) * 128])

        def s7():
            P.op("dve", lambda e: e.tensor_tensor(tA, psv, k.bsbc_t[:], ALU.add), reads=b01 + [k.bsbc], writes=t01)

        def s8():
            P.op("dve", lambda e: e.tensor_tensor(tB3, tA3, uap, ALU.mult), reads=t01 + at[0:8], writes=t23)

        def s9():
            P.op("act", lambda e: e.activation(sq2, tB, AF.Square), reads=t23, writes=sqb)

        def s10():
            mm(P, b01[0], k.ones_t[:], k.ones, sqb[0].ap, sqb[0], True, True)
            mm(P, b01[1], k.ones_t[:], k.ones, sqb[1].ap, sqb[1], True, True)

        def s11():
            P.op("act", lambda e: e.activation(tA, psv, AF.Ln, bias=k.eps_t[:, 0:1], scale=1.0 / 128),
                 reads=b01 + [k.eps], writes=t01)

        def s12():
            P.op("act", lambda e: e.activation(tA, tA, AF.Exp, scale=-0.5), reads=t01, writes=t01)

        def s13():
            for g in range(8):
                P.op("dve", lambda e, g=g: e.scalar_tensor_tensor(
                    out=k.yt_t[:, g, sub * 128:(sub + 1) * 128], in0=tB[:, g * 128:(g + 1) * 128],
                    scalar=col[:, C_GOA + g:C_GOA + g + 1], in1=tA[:, g * 128:(g + 1) * 128], op0=ALU.mult, op1=ALU.mult),
                    reads=t0123 + [k.cols], writes=[yt[g]])

        return [s1, s2, s3, s4, s5, s6, s7, s8, s9, s10, s11, s12, s13]

    for pair in ((0, 1), (2, 3)):
        L = [gm_stages(sb_) for sb_ in pair]
        for st_ in range(len(L[0])):
            for l_ in L:
                l_[st_]()
    tmp = k.tmp
    for cc in range(4):
        mm(P, ps[0], k.krsq.ap[0:64, cc * 128:(cc + 1) * 128], k.krsq, k.ones_t[0:64, 0:1], k.ones, True, True,
           out_ap=ps[0].ap[:, cc:cc + 1])
    P.op("act", lambda e: e.copy(k.ssr.ap, ps[0].ap[:, 0:4]), reads=[ps[0]], writes=[k.ssr])
    for hf in range(2):
        v, b = k.w4.acquire(k.wkall[hf], (128, 4, 512))
        if not P.dry:
            for cc in range(4):
                for kc in range(4):
                    mm(P, ps[hf * 4 + cc], k.ckv_t[:, kc, i * TT + cc * 128:i * TT + (cc + 1) * 128], k.ckv[kc][i],
                       v[:, kc, :], b, kc == 0, kc == 3)
        k.w4.release()
    junk = at[20]
    for cc in range(4):
        for hf in range(2):
            pb = ps[hf * 4 + cc]
            for hh in range(4):
                ccol = cc * 8 + hf * 4 + hh
                P.op("act", lambda e, pb=pb, hh=hh, ccol=ccol: e.activation(
                    junk.ap[:, 0:128], pb.ap[:, hh * 128:(hh + 1) * 128], AF.Square, accum_out=k.ssk_t[:, ccol:ccol + 1]),
                    reads=[pb], writes=[junk, k.ssk])
    for cc in range(4):
        P.op("dve", lambda e, cc=cc: e.tensor_scalar(k.ssk_t[:, cc * 8:(cc + 1) * 8], k.ssk_t[:, cc * 8:(cc + 1) * 8],
                                                     k.ssr_t[:, cc:cc + 1], None, ALU.add), reads=[k.ssk, k.ssr], writes=[k.ssk])
    P.op("act", lambda e: e.activation(k.sck[i].ap, k.ssk.ap, AF.Ln, bias=k.e192_t[:, 0:1], scale=1.0),
         reads=[k.ssk, k.e192], writes=[k.sck[i]])
    P.op("act", lambda e: e.activation(k.sck[i].ap, k.sck[i].ap, AF.Exp, scale=-0.5), reads=[k.sck[i]], writes=[k.sck[i]])
    pool_pages = list(ht[0:15]) + list(at[3:16])
    nb = i + 1
    ssz = 2 + 2 * nb
    D = 3 if 3 * ssz <= len(pool_pages) else 2
    sets = []
    for s_ in range(D):
        pg = pool_pages[s_ * ssz:(s_ + 1) * ssz]
        sets.append(dict(QN=pg[0], QR=pg[1], KN=pg[2:2 + nb], VV=pg[2 + nb:2 + 2 * nb]))
    PT = [at[0], at[1], at[2]]
    FSQ = ht[15]
    scale = 192.0 ** -0.5
    nchunk = 4 * i + 4
    SB = [ps[0], ps[1], ps[4]]
    LA = 2
    OB, LB, STB = ps[2], ps[3], ps[4]
    rt = [k.silu[0], k.silu[1]]

    def prep(h):
        S = sets[h % D]
        QN, QR, KN, VV = S["QN"], S["QR"], S["KN"], S["VV"]
        v, b = k.w4.acquire(k.wq[h], (128, 4, 256))
        if not P.dry:
            for kc in range(4):
                mm(P, ps[5], v[:, kc, 0:128], b, at[16 + kc].ap, at[16 + kc], kc == 0, kc == 3)
            for kc in range(4):
                mm(P, ps[6], v[:, kc, 128:192], b, at[16 + kc].ap, at[16 + kc], kc == 0, kc == 3, out_ap=ps[6].ap[0:64, :])
            for kc in range(4):
                mm(P, ps[7], v[:, kc, 192:256], b, at[16 + kc].ap, at[16 + kc], kc == 0, kc == 3, out_ap=ps[7].ap[0:64, :])
        k.w4.release()
        rope_pre(k, ps[6], ps[7], C_GQR, C_GQS, rt)
        rms_rstd(k, [(ps[5], ps[5].ap, 128), (ps[6], ps[6].ap[0:64, :], 64)], [at[20], at[21]], 192, STB, k.rstd)
        P.op("dve", lambda e: e.scalar_tensor_tensor(out=QN.ap, in0=ps[5].ap, scalar=col[:, C_GQN:C_GQN + 1], in1=k.rstd.ap,
                                                     op0=ALU.mult, op1=ALU.mult), reads=[ps[5], k.cols, k.rstd], writes=[QN])
        rope_post(k, QR.ap[0:64, :], [QR], k.rstd.ap[0:64, :], k.rstd, rt)
        v, b = k.w4.acquire(k.wkv[h], (128, 4, 256))
        if not P.dry:
            for j in range(i + 1):
                kb = ps[7] if j % 2 == 0 else ps[5]
                for kc in range(4):
                    mm(P, kb, v[:, kc, 0:128], b, k.ckv[kc][j].ap, k.ckv[kc][j], kc == 0, kc == 3)
                for cc in range(4):
                    c = 4 * j + cc
                    for kc in range(4):
                        mm(P, ps[6], k.ckv_t[:, kc, c * 128:(c + 1) * 128], k.ckv[kc][j], v[:, kc, 128:256], b, kc == 0, kc == 3,
                           out_ap=ps[6].ap[:, cc * 128:(cc + 1) * 128])
                P.op("dve", lambda e, j=j, kb=kb: e.tensor_scalar(KN[j].ap, kb.ap, col[:, C_GKN:C_GKN + 1], None, ALU.mult),
                     reads=[kb, k.cols], writes=[KN[j]])
                P.op("act", lambda e, j=j: e.copy(VV[j].ap, ps[6].ap), reads=[ps[6]], writes=[VV[j]])
        k.w4.release()

    def scores(h):
        S = sets[h % D]
        QN, QR, KN, VV = S["QN"], S["QR"], S["KN"], S["VV"]

        def geo(c):
            j, cc = divmod(c, 4)
            r = c - 4 * i
            lo = 128 * r if r > 0 else 0
            return j, cc, r, lo

        def s_mm(c):
            j, cc, r, lo = geo(c)
            sb = SB[c % 3]
            sap = sb.ap[:, lo:TT]
            mm(P, sb, KN[j].ap[:, cc * 128:(cc + 1) * 128], KN[j], QN.ap[:, lo:TT], QN, True, False, out_ap=sap)
            mm(P, sb, k.krb[j].ap[:, cc * 128:(cc + 1) * 128], k.krb[j], QR.ap[:, lo:TT], QR, False, True, out_ap=sap)

        for c0 in range(min(LA, nchunk)):
            s_mm(c0)
        for c in range(nchunk):
            j, cc, r, lo = geo(c)
            if c + LA < nchunk:
                s_mm(c + LA)
            sb = SB[c % 3]
            pt = PT[c % 3]
            sap = sb.ap[:, lo:TT]
            pap = pt.ap[:, lo:TT]
            scap = k.sck_t[:, (j * 4 + cc) * 8 + h:(j * 4 + cc) * 8 + h + 1]
            P.op("act", lambda e, sap=sap, pap=pap, scap=scap: e.activation(pap, sap, AF.Exp, scale=scap),
                 reads=[sb, k.sck[j]], writes=[pt])
            if r >= 0:
                dap = pt.ap[:, lo:lo + 128]
                P.op("dve", lambda e, dap=dap: e.tensor_tensor(dap, dap, k.mask.ap, ALU.mult), reads=[pt, k.mask], writes=[pt])
            mm(P, OB, VV[j].ap[:, cc * 128:(cc + 1) * 128], VV[j], pap, pt, c == 0, c == nchunk - 1, out_ap=OB.ap[:, lo:TT])
            mm(P, LB, k.ones_t[:], k.ones, pap, pt, c == 0, c == nchunk - 1, out_ap=LB.ap[:, lo:TT])

    def fin_a(h):
        P.op("act", lambda e: e.activation(tmp[2].ap, LB.ap, AF.Ln), reads=[LB], writes=[tmp[2]])
        P.op("act", lambda e: e.activation(tmp[2].ap, tmp[2].ap, AF.Exp, scale=-1.0), reads=[tmp[2]], writes=[tmp[2]])
        P.op("dve", lambda e: e.tensor_tensor(tmp[3].ap, OB.ap, tmp[2].ap, ALU.mult), reads=[OB, tmp[2]], writes=[tmp[3]])
        P.op("act", lambda e: e.activation(FSQ.ap, tmp[3].ap, AF.Square), reads=[tmp[3]], writes=[FSQ])

    def fin_b(h):
        mm(P, STB, k.ones_t[:], k.ones, FSQ.ap, FSQ, True, True)
        P.op("act", lambda e: e.activation(tmp[2].ap, STB.ap, AF.Ln, bias=k.eps_t[:, 0:1], scale=1.0 / 128),
             reads=[STB, k.eps], writes=[tmp[2]])
        P.op("act", lambda e: e.activation(tmp[2].ap, tmp[2].ap, AF.Exp, scale=-0.5), reads=[tmp[2]], writes=[tmp[2]])
        P.op("dve", lambda e, h=h: e.scalar_tensor_tensor(out=yt[8 + h].ap, in0=tmp[3].ap, scalar=col[:, C_GOB + h:C_GOB + h + 1],
                                                          in1=tmp[2].ap, op0=ALU.mult, op1=ALU.mult),
             reads=[tmp[3], k.cols, tmp[2]], writes=[yt[8 + h]])

    for h0 in range(D):
        prep(h0)
    for h in range(8):
        scores(h)
        fin_a(h)
        if h + D < 8:
            prep(h + D)
        fin_b(h)
    for dcol in range(DC):
        v, b = k.w4.acquire(k.wo[dcol], (128, DC, 128))
        pb = ps[4 + dcol % 4]
        if not P.dry:
            for cc in range(DC):
                mm(P, pb, v[:, cc, :], b, yt[cc].ap, yt[cc], cc == 0, cc == DC - 1)
        k.w4.release()
        x = k.xt[dcol]
        P.op("dve", lambda e, x=x, pb=pb: e.tensor_tensor(x.ap, x.ap, pb.ap, ALU.add), reads=[x, pb], writes=[x])


def program(k):
    P = k.P
    prologue(k)
    k.w4.prefetch()

    def pages(t):
        return [k.xslots[(c - 4 * t) % NXS] for c in range(DC)]

    def grp(pg, c0):
        s0 = pg[c0].slot
        assert [p_.slot for p_ in pg[c0:c0 + 4]] == [s0, s0 + 1, s0 + 2, s0 + 3]
        return k.xt_t[:, s0:s0 + 4, :]

    k.xt = pages(0)
    for g in range(4):
        P.dma("sp", grp(k.xt, 4 * g), k.x_d[0, :, 4 * g:4 * g + 4, :], writes=k.xt[4 * g:4 * g + 4])
    for t in range(k.n_tiles):
        k.xt = cur = pages(t)
        nxt = pages(t + 1) if t + 1 < k.n_tiles else None
        if "ffn1" in k.stages:
            ffn(k, 0, k.w1g, k.w1u, k.w1d)
        if "mix" in k.stages:
            mixer(k, t)

        def hook(g, t=t, cur=cur, nxt=nxt):
            P.dma("sp", k.o_d[t, :, 4 * g:4 * g + 4, :], grp(cur, 4 * g), reads=cur[4 * g:4 * g + 4])
            if nxt is not None and g < 3:
                P.dma("act", grp(nxt, 4 * g + 4), k.x_d[t + 1, :, 4 * g + 4:4 * g + 8, :], writes=nxt[4 * g + 4:4 * g + 8])

        if nxt is not None:
            P.dma("sp", grp(nxt, 0), k.x_d[t + 1, :, 0:4, :], writes=nxt[0:4])
        if "ffn2" in k.stages:
            ffn(k, 2, k.w2g, k.w2u, k.w2d, hook=hook)
        else:
            for g in range(4):
                hook(g)
    P.wait_bufs("sp", k.xslots)


def build(n_tiles=TILES_PER_CORE, stages=("ffn1", "mix", "ffn2")):
    nc = bass.Bass("TRN2", target_bir_lowering=False)
    P = Prog(nc)
    k = setup(nc, P, n_tiles, stages)
    P.dry = True
    program(k)
    P.dry = False
    k.gu_ctr = 0
    k.dg_ctr = 0
    for r in (k.w4, k.wd):
        r.n_acq = 0
        r.n_loaded = 0
    program(k)
    P.emit()
    return nc, P


def _lay_w_in_out(w):
    Kd, N = w.shape
    return np.ascontiguousarray(w.reshape(Kd // 128, 128, N // 128, 128).transpose(2, 1, 0, 3))


def _lay_wd(w):
    return np.ascontiguousarray(w.reshape(NFC, 128, 4, 512).transpose(2, 1, 0, 3))


def _lay_gain(g):
    return g.reshape(DC, 128).T


def host_inputs(inputs, n_tiles=TILES_PER_CORE, ncores=NCORES, mix=True):
    x = np.asarray(inputs["x"], dtype=np.float32)
    B, S, _ = x.shape
    xt = x.reshape(B * S // TT, TT, DC, 128).transpose(0, 3, 2, 1)
    shared = {
        "w1g": _lay_w_in_out(np.asarray(inputs["ffn1_w_gate"][0])),
        "w1u": _lay_w_in_out(np.asarray(inputs["ffn1_w_up"][0])),
        "w1d": _lay_wd(np.asarray(inputs["ffn1_w_down"][0])),
        "w2g": _lay_w_in_out(np.asarray(inputs["ffn2_w_gate"][0])),
        "w2u": _lay_w_in_out(np.asarray(inputs["ffn2_w_up"][0])),
        "w2d": _lay_wd(np.asarray(inputs["ffn2_w_down"][0])),
        "gains": np.ascontiguousarray(np.concatenate([
            _lay_gain(np.asarray(inputs["ffn1_norm_g"][0])),
            _lay_gain(np.asarray(inputs["mix_norm_g"][0])),
            _lay_gain(np.asarray(inputs["ffn2_norm_g"][0]))], axis=1).astype(np.float32)),
    }
    if mix:
        f32 = np.float32
        w_in = np.asarray(inputs["w_in"][0], dtype=f32)
        partner = (np.arange(64) + 32) % 64
        kr = w_in[:, 3072:3136]
        pad = np.zeros((D, 64), f32)
        w_in_x = np.concatenate([w_in[:, :3072], kr, pad, kr[:, partner], pad], axis=1)
        shared["win"] = _lay_w_in_out(w_in_x)
        wq = np.asarray(inputs["mla_w_q_up"][0], dtype=f32).reshape(512, 8, 192)
        wq_x = np.concatenate([wq, wq[:, :, 128 + partner]], axis=2)
        shared["wq"] = np.ascontiguousarray(wq_x.reshape(4, 128, 8, 256).transpose(2, 1, 0, 3))
        wkv = np.asarray(inputs["mla_w_kv_up"][0], dtype=f32).reshape(512, 8, 256)
        shared["wkv"] = np.ascontiguousarray(wkv.reshape(4, 128, 8, 256).transpose(2, 1, 0, 3))
        shared["wo"] = _lay_w_in_out(np.asarray(inputs["w_out"][0], dtype=f32))
        wk = wkv[:, :, 0:128].reshape(4, 128, 2, 4 * 128)
        shared["wkall"] = np.ascontiguousarray(wk.transpose(2, 1, 0, 3))
        cols = np.zeros((128, NCOL), f32)
        cols[:, C_GQ:C_GQ + 4] = np.asarray(inputs["mla_q_norm_g"][0]).reshape(4, 128).T
        cols[:, C_GKV:C_GKV + 4] = np.asarray(inputs["mla_kv_norm_g"][0]).reshape(4, 128).T
        qg = np.asarray(inputs["mla_q_head_g"][0])
        kg = np.asarray(inputs["mla_k_head_g"][0])
        cols[:, C_GQN] = qg[:128]
        cols[:64, C_GQR] = qg[128:]
        cols[:64, C_GQS] = qg[128 + partner]
        cols[:, C_GKN] = kg[:128]
        cols[:64, C_GKR] = kg[128:]
        cols[:64, C_GKS] = kg[128 + partner]
        cols[:, C_GOA:C_GOA + 8] = np.asarray(inputs["gmlp_out_g"][0]).T
        cols[:, C_GOB:C_GOB + 8] = np.asarray(inputs["mla_out_g"][0]).T
        half = 32
        inv_freq = (1.0 / (10000.0 ** (np.arange(half, dtype=f32) / half))).astype(f32)
        cols[:64, C_IF] = inv_freq[np.arange(64) % 32]
        cols[:64, C_SG] = np.where(np.arange(64) < 32, -1.0, 1.0)
        shared["cols"] = cols
        shared["gvbc"] = np.ascontiguousarray(np.asarray(inputs["gmlp_v_norm_g"][0], dtype=f32).reshape(1, 1024))
        shared["bsbc"] = np.ascontiguousarray(np.asarray(inputs["gmlp_b_s"][0], dtype=f32).reshape(1, 1024))
        ws = np.asarray(inputs["gmlp_w_s"][0], dtype=f32)
        shared["wst"] = np.ascontiguousarray(ws.transpose(2, 0, 1).reshape(128, 1024))
        pos = np.asarray(inputs["positions"]).astype(np.int32).reshape(-1, 1, TT)
    in_maps = []
    for c in range(ncores):
        m = dict(shared)
        m["xT"] = np.ascontiguousarray(xt[c * n_tiles:(c + 1) * n_tiles])
        if mix:
            m["pos"] = np.ascontiguousarray(pos[c * n_tiles:(c + 1) * n_tiles])
        in_maps.append(m)
    return in_maps


def kernel(**inputs):
    nc, P = build()
    in_maps = host_inputs(inputs)
    res = run_bass_kernel_spmd(nc, in_maps, core_ids=list(range(NCORES)))
    outs = [r["oT"] for r in res.results]
    o = np.concatenate(outs, axis=0)
    o = o.transpose(0, 3, 2, 1).reshape(16, 2048, D)
    return np.ascontiguousarray(o.astype(np.float32))
```

```python
import contextlib
import numpy as np
import concourse.bass as bass
import concourse.mybir as mybir
from concourse.bass_utils import run_bass_kernel_spmd

F32 = mybir.dt.float32
BF16 = mybir.dt.bfloat16
I32 = mybir.dt.int32
AF = mybir.ActivationFunctionType
ALU = mybir.AluOpType

SEM_CAP = 12000

D = 2048
DC = 16
FF = 5504
NFC = 43
TT = 512
QUARTERS = [(0, 11), (11, 11), (22, 11), (33, 10)]
EPS = 1e-6
NCORES = 8
TILES_PER_CORE = 8
NXS = 20
NCOL = 32
C_GQ, C_GKV, C_GQN, C_GQR, C_GQS, C_GKN, C_GKR, C_GKS, C_GOA, C_GOB, C_IF, C_SG = 0, 4, 8, 9, 10, 11, 12, 13, 14, 22, 30, 31
PI = 3.141592653589793


class Chan:
    def __init__(self, name, step):
        self.name = name
        self.step = step
        self.count = 0
        self.needed = set()
        self.sems = []
        self.pos = None


class Buf:
    def __init__(self, name, ap, excl=False):
        self.name = name
        self.ap = ap
        self.last_write = None
        self.readers = {}
        self.chan = None
        self.excl = excl


class Op:
    __slots__ = ("fn", "chan", "idx", "deps")

    def __init__(self, fn, chan, idx, deps):
        self.fn = fn
        self.chan = chan
        self.idx = idx
        self.deps = deps


class Prog:
    ENGINES = ("pe", "act", "dve", "pool", "sp")

    def __init__(self, nc, dry=False):
        self.nc = nc
        self.dry = dry
        self.ops = {e: [] for e in self.ENGINES}
        self.echan = {e: Chan("e_" + e, 1) for e in self.ENGINES}
        self.seen = {e: {} for e in self.ENGINES}
        self.chans = list(self.echan.values())
        self.stack = contextlib.ExitStack()

    def sbuf(self, name, shape, dtype):
        return self.stack.enter_context(self.nc.sbuf_tensor(name, shape, dtype))

    def psum(self, name, shape, dtype):
        return self.stack.enter_context(self.nc.psum_tensor(name, shape, dtype))

    def dma_chan(self, t):
        if t.chan is None:
            t.chan = Chan("d_" + t.name, 16)
            self.chans.append(t.chan)
        return t.chan

    def _add(self, engine, fn, reads, writes, chan):
        if self.dry:
            return None
        idx = chan.count
        chan.count += 1
        if any(t.excl for t in reads):
            writes = list(writes) + [t for t in reads if t.excl and t not in writes]
            reads = [t for t in reads if not t.excl]
        deps = []
        for t in reads:
            if t.last_write is not None:
                deps.append(t.last_write)
        for t in writes:
            if t.last_write is not None:
                deps.append(t.last_write)
            deps.extend(t.readers.items())
        seen = self.seen[engine]
        own = self.echan[engine]
        final = {}
        for (c, i) in deps:
            if c is own and engine == "pe":
                continue
            if seen.get(c, -1) >= i:
                continue
            if final.get(c, -1) < i:
                final[c] = i
        for c, i in final.items():
            seen[c] = i
            c.needed.add(i)
        op = Op(fn, chan, idx, list(final.items()))
        self.ops[engine].append(op)
        for t in reads:
            if t.readers.get(chan, -1) < idx:
                t.readers[chan] = idx
        for t in writes:
            t.last_write = (chan, idx)
            t.readers = {}
        return op

    def op(self, engine, fn, reads=(), writes=()):
        return self._add(engine, fn, reads, writes, self.echan[engine])

    def dma(self, engine, out_ap, in_ap, reads=(), writes=(), chan_buf=None):
        if self.dry:
            return None
        if chan_buf is None:
            chan_buf = writes[0] if writes else reads[0]
        chan = self.dma_chan(chan_buf)
        chan.needed.add(chan.count)

        def fn(eng):
            return eng.dma_start(out=out_ap, in_=in_ap)

        return self._add(engine, fn, reads, writes, chan)

    def wait_bufs(self, engine, bufs):
        return self._add(engine, lambda eng: None, [], list(bufs), self.echan[engine])

    def emit(self):
        nc = self.nc
        st = self.stack
        for c in self.chans:
            needed = sorted(c.needed)
            c.pos = {i: p for p, i in enumerate(needed)}
            nsem = (len(needed) + SEM_CAP - 1) // SEM_CAP
            c.sems = [st.enter_context(nc.semaphore(f"{c.name}_{k}")) for k in range(nsem)]
        self.nsems = sum(len(c.sems) for c in self.chans)

        def semval(c, i):
            p = c.pos[i]
            return c.sems[p // SEM_CAP], (p % SEM_CAP + 1) * c.step

        def run(engine, eng):
            for op in self.ops[engine]:
                for (c, i) in op.deps:
                    s, v = semval(c, i)
                    eng.wait_ge(s, v)
                ins = op.fn(eng)
                if ins is not None and op.idx in op.chan.needed:
                    s, v = semval(op.chan, op.idx)
                    ins.then_inc(s, op.chan.step)

        block = st.enter_context(nc.Block())

        @block.tensor
        def _(eng):
            run("pe", eng)

        @block.scalar
        def _(eng):
            run("act", eng)

        @block.vector
        def _(eng):
            run("dve", eng)

        @block.gpsimd
        def _(eng):
            run("pool", eng)

        @block.sync
        def _(eng):
            run("sp", eng)

    def close(self):
        self.stack.close()


class Ring:
    def __init__(self, P, name, nslots, slot_elems):
        self.P = P
        self.name = name
        self.n = nslots
        self.slots = []
        for i in range(nslots):
            t = P.sbuf(f"{name}{i}", [128, slot_elems], BF16)
            self.slots.append((t, Buf(f"{name}{i}", t[:])))
        self.reqs = []
        self.n_acq = 0
        self.n_loaded = 0

    def view(self, i, shape):
        t = self.slots[i % self.n][0]
        n = 1
        for s in shape[1:]:
            n *= s
        ap = t[0:shape[0], 0:n]
        if len(shape) == 3:
            ap = ap.rearrange("p (a b) -> p a b", a=shape[1])
        return ap

    def _load(self, i):
        src, shape = self.reqs[i]
        buf = self.slots[i % self.n][1]
        self.P.dma("pool", self.view(i, shape), src, writes=[buf])

    def acquire(self, src_ap, shape):
        P = self.P
        i = self.n_acq
        self.n_acq += 1
        if P.dry:
            self.reqs.append((src_ap, tuple(shape)))
            return None, None
        assert self.reqs[i][1] == tuple(shape)
        while self.n_loaded <= i:
            self._load(self.n_loaded)
            self.n_loaded += 1
        return self.view(i, shape), self.slots[i % self.n][1]

    def acquire_idx(self, src_ap, shape):
        i = self.n_acq
        v, b = self.acquire(src_ap, shape)
        return v, b, i

    def prefetch(self):
        if self.P.dry:
            return
        while self.n_loaded < min(self.n, len(self.reqs)):
            self._load(self.n_loaded)
            self.n_loaded += 1

    def release(self, i=None):
        if self.P.dry:
            return
        if i is None:
            i = self.n_acq - 1
        nxt = i + self.n
        if nxt < len(self.reqs) and self.n_loaded == nxt:
            self._load(nxt)
            self.n_loaded += 1


class K:
    pass


def setup(nc, P, n_tiles, stages):
    k = K()
    k.nc, k.P, k.n_tiles, k.stages = nc, P, n_tiles, stages
    dt = lambda name, shape, dtype=F32, kind="ExternalInput": nc.dram_tensor(name, shape, dtype, kind=kind).ap()
    k.x_d = dt("xT", [n_tiles, 128, DC, TT])
    k.o_d = dt("oT", [n_tiles, 128, DC, TT], kind="ExternalOutput")
    if "ffn1" in stages:
        k.w1g = dt("w1g", [NFC, 128, DC, 128])
        k.w1u = dt("w1u", [NFC, 128, DC, 128])
        k.w1d = dt("w1d", [4, 128, NFC, 512])
    if "ffn2" in stages:
        k.w2g = dt("w2g", [NFC, 128, DC, 128])
        k.w2u = dt("w2u", [NFC, 128, DC, 128])
        k.w2d = dt("w2d", [4, 128, NFC, 512])
    k.gains_d = dt("gains", [128, 3 * DC])

    xt = P.sbuf("xt", [128, NXS, TT], F32)
    k.xslots = [Buf(f"xt{c}", xt[:, c, :]) for c in range(NXS)]
    k.xt_t = xt
    for c_, b_ in enumerate(k.xslots):
        b_.slot = c_
    k.xt = k.xslots[0:DC]
    ht = P.sbuf("ht", [128, DC, TT], BF16)
    k.ht = [Buf(f"ht{c}", ht[:, c, :]) for c in range(DC)]
    at = P.sbuf("at", [128, 22, TT], BF16)
    k.at = [Buf(f"at{c}", at[:, c, :]) for c in range(22)]
    st = P.sbuf("tmp2", [128, 4 * TT], F32)
    k.tmp2_t = st
    k.tmp2 = [Buf(f"tmp2_{c}", st[:, c * TT:(c + 1) * TT]) for c in range(4)]
    k.silu = k.tmp2[0:2]
    k.rstd = k.tmp2[2]
    gains = P.sbuf("gains_sb", [128, 3 * DC], F32)
    k.gains = Buf("gains", gains[:])
    k.gains_t = gains
    eps_t = P.sbuf("eps", [128, 1], F32)
    k.eps = Buf("eps", eps_t[:])
    k.eps_t = eps_t
    ones = P.sbuf("ones", [128, 128], BF16)
    k.ones = Buf("ones", ones[:])
    k.ones_t = ones
    k.w4 = Ring(P, "w4_", 6, 2048)
    k.wd = Ring(P, "wd_", 3, 11 * 512)
    psall = P.psum("psall", [128, 8 * 512], F32)
    k.ps_all = psall
    k.ps = [Buf(f"ps{i}", psall[:, i * 512:(i + 1) * 512], excl=True) for i in range(8)]
    k.gu_ctr = 0
    k.dg_ctr = 0
    k.ht_t, k.at_t = ht, at
    if "mix" in stages:
        k.win = dt("win", [26, 128, DC, 128])
        k.wq = dt("wq", [8, 128, 4, 256])
        k.wkv = dt("wkv", [8, 128, 4, 256])
        k.wo = dt("wo", [DC, 128, DC, 128])
        k.cols_d = dt("cols", [128, NCOL])
        k.gvbc_d = dt("gvbc", [1, 1024])
        k.bsbc_d = dt("bsbc", [1, 1024])
        k.wst_d = dt("wst", [128, 1024])
        k.pos_d = dt("pos", [n_tiles, 1, TT], I32)
        cols = P.sbuf("cols_sb", [128, NCOL], F32)
        k.cols_t, k.cols = cols, Buf("cols", cols[:])
        gvbc = P.sbuf("gvbc_sb", [128, 1024], F32)
        k.gvbc_t, k.gvbc = gvbc, Buf("gvbc", gvbc[:])
        bsbc = P.sbuf("bsbc_sb", [128, 1024], F32)
        k.bsbc_t, k.bsbc = bsbc, Buf("bsbc", bsbc[:])
        wct = P.sbuf("wct", [128, 1024], BF16)
        k.wct_t, k.wct = wct, Buf("wct", wct[:])
        mask = P.sbuf("mask01", [128, 128], BF16)
        k.mask_t, k.mask = mask, Buf("mask", mask[:])
        tmp = P.sbuf("tmp", [128, 4 * TT], F32)
        k.tmp_t = tmp
        k.tmp = [Buf(f"tmp{j}", tmp[:, j * TT:(j + 1) * TT]) for j in range(4)]
        yt = P.sbuf("yt", [128, DC, TT], BF16)
        k.yt_t = yt
        k.yt = [Buf(f"yt{c}", yt[:, c, :]) for c in range(DC)]
        ckv = P.sbuf("ckv", [128, 4, 2048], BF16)
        k.ckv_t = ckv
        k.ckv = [[Buf(f"ckv{kc}_{j}", ckv[:, kc, j * TT:(j + 1) * TT]) for j in range(4)] for kc in range(4)]
        krb = P.sbuf("krb", [128, 2048], BF16)
        k.krb_t = krb
        k.krb = [Buf(f"krb{j}", krb[:, j * TT:(j + 1) * TT]) for j in range(4)]
        krsq = P.sbuf("krsq", [64, TT], BF16)
        k.krsq = Buf("krsq", krsq[:])
        sck = P.sbuf("sck", [128, 128], F32)
        k.sck_t = sck
        k.sck = [Buf(f"sck{j}", sck[:, j * 32:(j + 1) * 32]) for j in range(4)]
        ssk = P.sbuf("ssk", [128, 32], F32)
        k.ssk_t, k.ssk = ssk, Buf("ssk", ssk[:])
        ssr = P.sbuf("ssr", [128, 4], F32)
        k.ssr_t, k.ssr = ssr, Buf("ssr", ssr[:])
        e192 = P.sbuf("e192", [128, 1], F32)
        k.e192_t, k.e192 = e192, Buf("e192", e192[:])
        k.wkall = dt("wkall", [2, 128, 4, 512])
        cs = P.sbuf("cos_sb", [64, TT], F32)
        k.cos = Buf("cos", cs[:])
        sn = P.sbuf("sin_sb", [64, TT], F32)
        k.sin = Buf("sin", sn[:])
        posi = P.sbuf("posi", [64, TT], I32)
        k.posi = Buf("posi", posi[:])
        ki = P.sbuf("ki", [64, TT], I32)
        k.ki = Buf("ki", ki[:])
        ssc = P.sbuf("ssc", [128, 8], F32)
        k.ssc_t, k.ssc = ssc, Buf("ssc", ssc[:])
        k.sscb = [Buf(f"ssc{j}", ssc[:, j:j + 1]) for j in range(4)]
    return k


def prologue(k):
    P = k.P
    P.dma("sp", k.gains.ap, k.gains_d, writes=[k.gains])
    P.op("dve", lambda e: e.memset(k.ones.ap, 1.0), writes=[k.ones])
    P.op("dve", lambda e: e.memset(k.eps.ap, EPS), writes=[k.eps])
    if "mix" in k.stages:
        P.op("dve", lambda e: e.memset(k.e192.ap, 192.0 * EPS), writes=[k.e192])
        P.op("dve", lambda e: e.memset(k.krb_t[64:128, :], 0.0), writes=list(k.krb))
        P.dma("sp", k.cols.ap, k.cols_d, writes=[k.cols])
        P.dma("sp", k.gvbc.ap, k.gvbc_d.partition_broadcast(128), writes=[k.gvbc])
        P.dma("sp", k.bsbc.ap, k.bsbc_d.partition_broadcast(128), writes=[k.bsbc])
        wsf = k.tmp_t[:, 0:1024]
        P.dma("sp", wsf, k.wst_d, writes=[k.tmp[0], k.tmp[1]])
        pat = [[0, 8], [1, 128]]
        P.op("pool", lambda e: e.affine_select(wsf.rearrange("p (a b) -> p a b", a=8), wsf.rearrange("p (a b) -> p a b", a=8),
                                               pat, ALU.is_ge, 0.0, base=0, channel_multiplier=-1),
             reads=[k.tmp[0], k.tmp[1]], writes=[k.tmp[0], k.tmp[1]])
        P.op("dve", lambda e: e.tensor_copy(k.wct.ap, wsf), reads=[k.tmp[0], k.tmp[1]], writes=[k.wct])
        P.op("pool", lambda e: e.affine_select(k.mask.ap, k.ones.ap, [[1, 128]], ALU.is_ge, 0.0, base=0, channel_multiplier=-1),
             reads=[k.ones], writes=[k.mask])


def mm(P, out, lhsT_ap, lhsT_buf, rhs_ap, rhs_buf, start, stop, out_ap=None):
    oap = out.ap if out_ap is None else out_ap
    P.op("pe", lambda e: e.matmul(oap, lhsT_ap, rhs_ap, start=start, stop=stop),
         reads=[lhsT_buf, rhs_buf], writes=[out])


def rms_rstd(k, srcs, sq_bufs, n_feat, ps, rstd, rows=None):
    P = k.P
    n = len(srcs)
    for i in range(n):
        sb, sap, nr = srcs[i][:3]
        if len(srcs[i]) > 3 and srcs[i][3]:
            mm(P, ps, k.ones_t[0:nr, :], k.ones, sap, sb, i == 0, i == n - 1)
            continue
        q = sq_bufs[i]
        qap = q.ap[0:nr, :]
        P.op("act", lambda e, sap=sap, qap=qap: e.activation(qap, sap, AF.Square), reads=[sb], writes=[q])
        mm(P, ps, k.ones_t[0:nr, :], k.ones, qap, q, i == 0, i == n - 1)
    P.op("act", lambda e: e.activation(rstd.ap, ps.ap, AF.Ln, bias=k.eps_t[:, 0:1], scale=1.0 / n_feat),
         reads=[ps, k.eps], writes=[rstd])
    P.op("act", lambda e: e.activation(rstd.ap, rstd.ap, AF.Exp, scale=-0.5), reads=[rstd], writes=[rstd])


def norm_to_ht(k, gidx):
    P = k.P
    gt = k.gains_t
    rms_rstd(k, [(b, b.ap, 128) for b in k.xt], k.at[:DC], D, k.ps[4], k.rstd)
    for c in range(DC):
        x, h = k.xt[c], k.ht[c]
        gap = gt[:, gidx * DC + c: gidx * DC + c + 1]
        P.op("dve", lambda e, x=x, h=h, gap=gap: e.scalar_tensor_tensor(
            out=h.ap, in0=x.ap, scalar=gap, in1=k.rstd.ap, op0=ALU.mult, op1=ALU.mult),
            reads=[x, k.rstd, k.gains], writes=[h])


def ffn(k, gidx, wg, wu, wd, hook=None):
    P = k.P
    norm_to_ht(k, gidx)

    def gu_phase(q):
        f0, nf = QUARTERS[q]
        s = (q % 2) * 11
        fstart = 0
        if q == 0:
            items = []
            for fi in (0, 1):
                pg = k.ps[(k.gu_ctr % 2) * 2]
                pu = k.ps[(k.gu_ctr % 2) * 2 + 1]
                sl = k.silu[k.gu_ctr % 2]
                k.gu_ctr += 1
                vg, bg, ig = k.w4.acquire_idx(wg[f0 + fi], (128, DC, 128))
                vu, bu, iu = k.w4.acquire_idx(wu[f0 + fi], (128, DC, 128))
                items.append((fi, pg, pu, sl, vg, bg, ig, vu, bu, iu))
            if not P.dry:
                for c in range(DC):
                    for (fi, pg, pu, sl, vg, bg, ig, vu, bu, iu) in items:
                        mm(P, pg, vg[:, c, :], bg, k.ht[c].ap, k.ht[c], c == 0, c == DC - 1)
                        mm(P, pu, vu[:, c, :], bu, k.ht[c].ap, k.ht[c], c == 0, c == DC - 1)
            for (fi, pg, pu, sl, vg, bg, ig, vu, bu, iu) in items:
                k.w4.release(ig)
                k.w4.release(iu)
            for (fi, pg, pu, sl, vg, bg, ig, vu, bu, iu) in items:
                a = k.at[s + fi]
                P.op("act", lambda e, sl=sl, pg=pg: e.activation(sl.ap, pg.ap, AF.Silu), reads=[pg], writes=[sl])
                P.op("dve", lambda e, a=a, sl=sl, pu=pu: e.tensor_tensor(a.ap, sl.ap, pu.ap, ALU.mult),
                     reads=[sl, pu], writes=[a])
            fstart = 2
        for fi in range(fstart, nf):
            f = f0 + fi
            pg = k.ps[(k.gu_ctr % 2) * 2]
            pu = k.ps[(k.gu_ctr % 2) * 2 + 1]
            sl = k.silu[k.gu_ctr % 2]
            k.gu_ctr += 1
            for (w, pb) in ((wg, pg), (wu, pu)):
                v, b = k.w4.acquire(w[f], (128, DC, 128))
                if not P.dry:
                    for c in range(DC):
                        mm(P, pb, v[:, c, :], b, k.ht[c].ap, k.ht[c], c == 0, c == DC - 1)
                k.w4.release()
            a = k.at[s + fi]
            P.op("act", lambda e, sl=sl, pg=pg: e.activation(sl.ap, pg.ap, AF.Silu), reads=[pg], writes=[sl])
            P.op("dve", lambda e, a=a, sl=sl, pu=pu: e.tensor_tensor(a.ap, sl.ap, pu.ap, ALU.mult),
                 reads=[sl, pu], writes=[a])

    def down_phase(q):
        f0, nf = QUARTERS[q]
        s = (q % 2) * 11
        for g in range(4):
            v, b = k.wd.acquire(wd[g, :, f0:f0 + nf, :], (128, nf, 512))
            pbase = 4 if k.dg_ctr % 2 == 0 else 0
            k.dg_ctr += 1
            if not P.dry:
                for fi in range(nf):
                    a = k.at[s + fi]
                    for dd in range(4):
                        mm(P, k.ps[pbase + dd], v[:, fi, dd * 128:(dd + 1) * 128], b, a.ap, a, fi == 0, fi == nf - 1)
            k.wd.release()
            for dd in range(4):
                x = k.xt[g * 4 + dd]
                pb = k.ps[pbase + dd]
                P.op("dve", lambda e, x=x, pb=pb: e.scalar_tensor_tensor(
                    out=x.ap, in0=pb.ap, scalar=0.5, in1=x.ap, op0=ALU.mult, op1=ALU.add),
                    reads=[pb, x], writes=[x])
            if hook is not None and q == 3:
                hook(g)

    gu_phase(0)
    k.wd.prefetch()
    gu_phase(1)
    down_phase(0)
    gu_phase(2)
    down_phase(1)
    gu_phase(3)
    down_phase(2)
    down_phase(3)


def rope_tables(k, t):
    P = k.P
    T = k.tmp_t
    col = k.cols_t
    P.dma("sp", k.posi.ap, k.pos_d[t].partition_broadcast(64), writes=[k.posi])
    pf, ang, tt, a = (T[0:64, j * TT:(j + 1) * TT] for j in range(4))
    tb = k.tmp
    P.op("dve", lambda e: e.tensor_copy(pf, k.posi.ap), reads=[k.posi], writes=[tb[0]])
    P.op("dve", lambda e: e.tensor_scalar(ang, pf, col[0:64, C_IF:C_IF + 1], None, ALU.mult),
         reads=[tb[0], k.cols], writes=[tb[1]])
    for (dst, sh_t, sh_a, is_sin) in ((k.sin, 0.5, 0.0, True), (k.cos, 0.75, PI / 2, False)):
        P.op("dve", lambda e, sh_t=sh_t: e.tensor_scalar(tt, ang, 1.0 / (2 * PI), sh_t, ALU.mult, ALU.add),
             reads=[tb[1]], writes=[tb[2]])
        P.op("dve", lambda e: e.tensor_copy(k.ki.ap, tt), reads=[tb[2]], writes=[k.ki])
        P.op("dve", lambda e: e.tensor_copy(tt, k.ki.ap), reads=[k.ki], writes=[tb[2]])
        P.op("dve", lambda e: e.scalar_tensor_tensor(out=a, in0=tt, scalar=-2 * PI, in1=ang, op0=ALU.mult, op1=ALU.add),
             reads=[tb[2], tb[1]], writes=[tb[3]])
        if sh_a != 0.0:
            P.op("dve", lambda e, sh_a=sh_a: e.tensor_scalar(a, a, sh_a, None, ALU.add), reads=[tb[3]], writes=[tb[3]])
        P.op("dve", lambda e: e.tensor_scalar(tt, a, -PI, 2 * PI, ALU.is_lt, ALU.mult), reads=[tb[3]], writes=[tb[2]])
        P.op("dve", lambda e: e.tensor_tensor(a, a, tt, ALU.add), reads=[tb[3], tb[2]], writes=[tb[3]])
        P.op("dve", lambda e: e.tensor_scalar(a, a, PI, -PI, ALU.min, ALU.max), reads=[tb[3]], writes=[tb[3]])
        if is_sin:
            P.op("act", lambda e, dst=dst: e.activation(dst.ap, a, AF.Sin, scale=col[0:64, C_SG:C_SG + 1]),
                 reads=[tb[3], k.cols], writes=[dst])
        else:
            P.op("act", lambda e, dst=dst: e.activation(dst.ap, a, AF.Sin), reads=[tb[3]], writes=[dst])


def rope_apply(k, ps_x, ps_sw, c_g, c_gs, dst_ap, dst_bufs, rstd_ap=None, rstd_buf=None, tbufs=None):
    P = k.P
    col = k.cols_t
    tb = k.tmp if tbufs is None else tbufs
    t1, t2 = tb[0].ap[0:64, :], tb[1].ap[0:64, :]
    P.op("dve", lambda e: e.scalar_tensor_tensor(out=t1, in0=ps_x.ap[0:64, :], scalar=col[0:64, c_g:c_g + 1], in1=k.cos.ap,
                                                 op0=ALU.mult, op1=ALU.mult), reads=[ps_x, k.cols, k.cos], writes=[tb[0]])
    P.op("dve", lambda e: e.scalar_tensor_tensor(out=t2, in0=ps_sw.ap[0:64, :], scalar=col[0:64, c_gs:c_gs + 1], in1=k.sin.ap,
                                                 op0=ALU.mult, op1=ALU.mult), reads=[ps_sw, k.cols, k.sin], writes=[tb[1]])
    if rstd_ap is None:
        P.op("dve", lambda e: e.tensor_tensor(dst_ap, t1, t2, ALU.add), reads=[tb[0], tb[1]], writes=dst_bufs)
    else:
        P.op("dve", lambda e: e.tensor_tensor(t1, t1, t2, ALU.add), reads=[tb[0], tb[1]], writes=[tb[0]])
        P.op("dve", lambda e: e.tensor_tensor(dst_ap, t1, rstd_ap, ALU.mult), reads=[tb[0], rstd_buf], writes=dst_bufs)


def rope_pre(k, ps_x, ps_sw, c_g, c_gs, tbufs):
    P = k.P
    col = k.cols_t
    tb = tbufs
    t1, t2 = tb[0].ap[0:64, :], tb[1].ap[0:64, :]
    P.op("dve", lambda e: e.scalar_tensor_tensor(out=t1, in0=ps_x.ap[0:64, :], scalar=col[0:64, c_g:c_g + 1], in1=k.cos.ap,
                                                 op0=ALU.mult, op1=ALU.mult), reads=[ps_x, k.cols, k.cos], writes=[tb[0]])
    P.op("dve", lambda e: e.scalar_tensor_tensor(out=t2, in0=ps_sw.ap[0:64, :], scalar=col[0:64, c_gs:c_gs + 1], in1=k.sin.ap,
                                                 op0=ALU.mult, op1=ALU.mult), reads=[ps_sw, k.cols, k.sin], writes=[tb[1]])
    P.op("dve", lambda e: e.tensor_tensor(t1, t1, t2, ALU.add), reads=[tb[0], tb[1]], writes=[tb[0]])


def rope_post(k, dst_ap, dst_bufs, rstd_ap, rstd_buf, tbufs):
    P = k.P
    t1 = tbufs[0].ap[0:64, :]
    P.op("dve", lambda e: e.tensor_tensor(dst_ap, t1, rstd_ap, ALU.mult), reads=[tbufs[0], rstd_buf], writes=dst_bufs)


def mixer(k, t):
    P = k.P
    i = t % 4
    col = k.cols_t
    ps, at, ht, tmp, yt = k.ps, k.at, k.ht, k.tmp, k.yt
    PA, T = k.ps_all, k.tmp_t
    norm_to_ht(k, 1)
    rope_tables(k, t)
    uitems = []
    for c in range(4):
        v, b, iu_ = k.w4.acquire_idx(k.win[c], (128, DC, 128))
        uitems.append((c, v, b, iu_))
    if not P.dry:
        for kc in range(DC):
            for (c, v, b, iu_) in uitems:
                mm(P, ps[c], v[:, kc, :], b, ht[kc].ap, ht[kc], kc == 0, kc == DC - 1)
    for (c, v, b, iu_) in uitems:
        k.w4.release(iu_)
    for (c, v, b, iu_) in uitems:
        P.op("act", lambda e, c=c: e.activation(at[c].ap, ps[c].ap, AF.Gelu), reads=[ps[c]], writes=[at[c]])
    for c in range(4, 8):
        v, b = k.w4.acquire(k.win[c], (128, DC, 128))
        pb = ps[c % 4]
        if not P.dry:
            for kc in range(DC):
                mm(P, pb, v[:, kc, :], b, ht[kc].ap, ht[kc], kc == 0, kc == DC - 1)
        k.w4.release()
        P.op("act", lambda e, pb=pb, c=c: e.activation(at[c].ap, pb.ap, AF.Gelu), reads=[pb], writes=[at[c]])
    for (c0, cg, dsts) in ((16, C_GQ, [at[16 + j] for j in range(4)]), (20, C_GKV, [k.ckv[j][i] for j in range(4)])):
        for j in range(4):
            v, b = k.w4.acquire(k.win[c0 + j], (128, DC, 128))
            pb = ps[j]
            if not P.dry:
                for kc in range(DC):
                    mm(P, pb, v[:, kc, :], b, ht[kc].ap, ht[kc], kc == 0, kc == DC - 1)
            k.w4.release()
            P.op("act", lambda e, pb=pb, j=j: e.copy(tmp[j].ap, pb.ap), reads=[pb], writes=[tmp[j]])
        rms_rstd(k, [(tmp[j], tmp[j].ap, 128) for j in range(4)], [at[20], at[21], at[20], at[21]], 512, ps[4], k.rstd)
        for j in range(4):
            P.op("dve", lambda e, j=j, d=dsts[j], cg=cg: e.scalar_tensor_tensor(
                out=d.ap, in0=tmp[j].ap, scalar=col[:, cg + j:cg + j + 1], in1=k.rstd.ap, op0=ALU.mult, op1=ALU.mult),
                reads=[tmp[j], k.cols, k.rstd], writes=[dsts[j]])
    for (c, pb) in ((24, ps[4]), (25, ps[5])):
        v, b = k.w4.acquire(k.win[c], (128, DC, 128))
        if not P.dry:
            for kc in range(DC):
                mm(P, pb, v[:, kc, 0:64], b, ht[kc].ap, ht[kc], kc == 0, kc == DC - 1, out_ap=pb.ap[0:64, :])
        k.w4.release()
    P.op("act", lambda e: e.activation(k.krsq.ap, ps[4].ap[0:64, :], AF.Square), reads=[ps[4]], writes=[k.krsq])
    rope_apply(k, ps[4], ps[5], C_GKR, C_GKS, k.krb[i].ap[0:64, :], [k.krb[i]])
    for c in range(8):
        v, b = k.w4.acquire(k.win[8 + c], (128, DC, 128))
        if not P.dry:
            for sub in range(4):
                bank = ps[sub * 2 + c // 4]
                oap = PA[:, sub * 1024 + c * 128: sub * 1024 + (c + 1) * 128]
                for kc in range(DC):
                    mm(P, bank, k.ht_t[:, kc, sub * 128:(sub + 1) * 128], ht[kc], v[:, kc, :], b, kc == 0, kc == DC - 1, out_ap=oap)
        k.w4.release()
    def gm_stages(sub):
        if sub % 2 == 0:
            tA, tB = T[:, 0:1024], T[:, 1024:2048]
            tmp = k.tmp
            sq2 = k.at_t[:, 20:22, :].rearrange("p a b -> p (a b)")
            sqb = [at[20], at[21]]
        else:
            tA, tB = k.tmp2_t[:, 0:1024], k.tmp2_t[:, 1024:2048]
            tmp = k.tmp2
            sq2 = k.ht_t[:, 0:2, :].rearrange("p a b -> p (a b)")
            sqb = [ht[0], ht[1]]
        b01 = [ps[2 * sub], ps[2 * sub + 1]]
        psv = PA[:, sub * 1024:(sub + 1) * 1024]
        ssc = k.ssc_t[:, sub:sub + 1]
        rsc = k.ssc_t[:, 4 + sub:5 + sub]
        sscb = k.sscb[sub]
        vtok = k.at_t[:, 8 + 2 * sub:10 + 2 * sub, :].rearrange("p a b -> p (a b)")
        vb = [at[8 + 2 * sub], at[9 + 2 * sub]]
        uap = k.at_t[:, 0:8, sub * 128:(sub + 1) * 128]
        tA3, tB3 = tA.rearrange("p (a b) -> p a b", a=8), tB.rearrange("p (a b) -> p a b", a=8)
        t01, t23, t0123 = [tmp[0], tmp[1]], [tmp[2], tmp[3]], [tmp[0], tmp[1], tmp[2], tmp[3]]

        def s1():
            P.op("act", lambda e: e.activation(tA, psv, AF.Gelu), reads=b01, writes=t01)

        def s2():
            P.op("act", lambda e: e.activation(sq2, tA, AF.Square, accum_out=ssc), reads=t01, writes=sqb + [sscb])

        def s3():
            P.op("act", lambda e: e.activation(rsc, ssc, AF.Ln, bias=k.eps_t[:, 0:1], scale=1.0 / 1024),
                 reads=[sscb, k.eps], writes=[sscb])

        def s4():
            P.op("act", lambda e: e.activation(rsc, rsc, AF.Exp, scale=-0.5), reads=[sscb], writes=[sscb])

        def s5():
            P.op("dve", lambda e: e.scalar_tensor_tensor(out=vtok, in0=tA, scalar=rsc, in1=k.gvbc_t[:], op0=ALU.mult, op1=ALU.mult),
                 reads=t01 + [sscb, k.gvbc], writes=vb)

        def s6():
            for g in range(8):
                lhs = k.at_t[:, 8 + 2 * sub + g // 4, (g % 4) * 128:(g % 4 + 1) * 128]
                mm(P, b01[g // 4], lhs, vb[g // 4], k.wct_t[:, g * 128:(g + 1) * 128], k.wct, True, True,
                   out_ap=psv[:, g * 128:(g + 1) * 128])

        def s7():
            P.op("dve", lambda e: e.tensor_tensor(tA, psv, k.bsbc_t[:], ALU.add), reads=b01 + [k.bsbc], writes=t01)

        def s8():
            P.op("dve", lambda e: e.tensor_tensor(tB3, tA3, uap, ALU.mult), reads=t01 + at[0:8], writes=t23)

        def s9():
            P.op("act", lambda e: e.activation(sq2, tB, AF.Square), reads=t23, writes=sqb)

        def s10():
            mm(P, b01[0], k.ones_t[:], k.ones, sqb[0].ap, sqb[0], True, True)
            mm(P, b01[1], k.ones_t[:], k.ones, sqb[1].ap, sqb[1], True, True)

        def s11():
            P.op("act", lambda e: e.activation(tA, psv, AF.Ln, bias=k.eps_t[:, 0:1], scale=1.0 / 128),
                 reads=b01 + [k.eps], writes=t01)

        def s12():
            P.op("act", lambda e: e.activation(tA, tA, AF.Exp, scale=-0.5), reads=t01, writes=t01)

        def s13():
            for g in range(8):
                P.op("dve", lambda e, g=g: e.scalar_tensor_tensor(
                    out=k.yt_t[:, g, sub * 128:(sub + 1) * 128], in0=tB[:, g * 128:(g + 1) * 128],
                    scalar=col[:, C_GOA + g:C_GOA + g + 1], in1=tA[:, g * 128:(g + 1) * 128], op0=ALU.mult, op1=ALU.mult),
                    reads=t0123 + [k.cols], writes=[yt[g]])

        return [s1, s2, s3, s4, s5, s6, s7, s8, s9, s10, s11, s12, s13]

    for pair in ((0, 1), (2, 3)):
        L = [gm_stages(sb_) for sb_ in pair]
        for st_ in range(len(L[0])):
            for l_ in L:
                l_[st_]()
    tmp = k.tmp
    for cc in range(4):
        mm(P, ps[0], k.krsq.ap[0:64, cc * 128:(cc + 1) * 128], k.krsq, k.ones_t[0:64, 0:1], k.ones, True, True,
           out_ap=ps[0].ap[:, cc:cc + 1])
    P.op("act", lambda e: e.copy(k.ssr.ap, ps[0].ap[:, 0:4]), reads=[ps[0]], writes=[k.ssr])
    for hf in (1, 0):
        v, b = k.w4.acquire(k.wkall[hf], (128, 4, 512))
        if not P.dry:
            for cc in range(4):
                for kc in range(4):
                    mm(P, ps[hf * 4 + cc], k.ckv_t[:, kc, i * TT + cc * 128:i * TT + (cc + 1) * 128], k.ckv[kc][i],
                       v[:, kc, :], b, kc == 0, kc == 3)
        k.w4.release()
    junk = at[20]
    for hf in (1, 0):
        for cc in range(4):
            pb = ps[hf * 4 + cc]
            for hh in range(4):
                ccol = cc * 8 + hf * 4 + hh
                P.op("act", lambda e, pb=pb, hh=hh, ccol=ccol: e.activation(
                    junk.ap[:, 0:128], pb.ap[:, hh * 128:(hh + 1) * 128], AF.Square, accum_out=k.ssk_t[:, ccol:ccol + 1]),
                    reads=[pb], writes=[junk, k.ssk])
    for cc in range(4):
        P.op("dve", lambda e, cc=cc: e.tensor_scalar(k.ssk_t[:, cc * 8:(cc + 1) * 8], k.ssk_t[:, cc * 8:(cc + 1) * 8],
                                                     k.ssr_t[:, cc:cc + 1], None, ALU.add), reads=[k.ssk, k.ssr], writes=[k.ssk])
    P.op("act", lambda e: e.activation(k.sck[i].ap, k.ssk.ap, AF.Ln, bias=k.e192_t[:, 0:1], scale=1.0),
         reads=[k.ssk, k.e192], writes=[k.sck[i]])
    P.op("act", lambda e: e.activation(k.sck[i].ap, k.sck[i].ap, AF.Exp, scale=-0.5), reads=[k.sck[i]], writes=[k.sck[i]])
    pool_pages = list(ht[0:15]) + list(at[3:16])
    nb = i + 1
    ssz = 2 + 2 * nb
    D = 3 if 3 * ssz <= len(pool_pages) else 2
    sets = []
    for s_ in range(D):
        pg = pool_pages[s_ * ssz:(s_ + 1) * ssz]
        sets.append(dict(QN=pg[0], QR=pg[1], KN=pg[2:2 + nb], VV=pg[2 + nb:2 + 2 * nb]))
    PT = [at[0], at[1], at[2]]
    FSQ = ht[15]
    scale = 192.0 ** -0.5
    nchunk = 4 * i + 4
    SB = [ps[0], ps[1], ps[4]]
    LA = 2
    OB, LB, STB = ps[2], ps[3], ps[4]
    rt = [k.silu[0], k.silu[1]]

    def prep(h):
        S = sets[h % D]
        QN, QR, KN, VV = S["QN"], S["QR"], S["KN"], S["VV"]
        v, b = k.w4.acquire(k.wq[h], (128, 4, 256))
        if not P.dry:
            for kc in range(4):
                mm(P, ps[5], v[:, kc, 0:128], b, at[16 + kc].ap, at[16 + kc], kc == 0, kc == 3)
            for kc in range(4):
                mm(P, ps[6], v[:, kc, 128:192], b, at[16 + kc].ap, at[16 + kc], kc == 0, kc == 3, out_ap=ps[6].ap[0:64, :])
            for kc in range(4):
                mm(P, ps[7], v[:, kc, 192:256], b, at[16 + kc].ap, at[16 + kc], kc == 0, kc == 3, out_ap=ps[7].ap[0:64, :])
        k.w4.release()
        rope_pre(k, ps[6], ps[7], C_GQR, C_GQS, rt)
        rms_rstd(k, [(ps[5], ps[5].ap, 128), (ps[6], ps[6].ap[0:64, :], 64)], [at[20], at[21]], 192, STB, k.rstd)
        P.op("dve", lambda e: e.scalar_tensor_tensor(out=QN.ap, in0=ps[5].ap, scalar=col[:, C_GQN:C_GQN + 1], in1=k.rstd.ap,
                                                     op0=ALU.mult, op1=ALU.mult), reads=[ps[5], k.cols, k.rstd], writes=[QN])
        rope_post(k, QR.ap[0:64, :], [QR], k.rstd.ap[0:64, :], k.rstd, rt)
        v, b = k.w4.acquire(k.wkv[h], (128, 4, 256))
        if not P.dry:
            for j in range(i + 1):
                kb = ps[7] if j % 2 == 0 else ps[5]
                for kc in range(4):
                    mm(P, kb, v[:, kc, 0:128], b, k.ckv[kc][j].ap, k.ckv[kc][j], kc == 0, kc == 3)
                for cc in range(4):
                    c = 4 * j + cc
                    for kc in range(4):
                        mm(P, ps[6], k.ckv_t[:, kc, c * 128:(c + 1) * 128], k.ckv[kc][j], v[:, kc, 128:256], b, kc == 0, kc == 3,
                           out_ap=ps[6].ap[:, cc * 128:(cc + 1) * 128])
                P.op("dve", lambda e, j=j, kb=kb: e.tensor_scalar(KN[j].ap, kb.ap, col[:, C_GKN:C_GKN + 1], None, ALU.mult),
                     reads=[kb, k.cols], writes=[KN[j]])
                P.op("act", lambda e, j=j: e.copy(VV[j].ap, ps[6].ap), reads=[ps[6]], writes=[VV[j]])
        k.w4.release()

    def scores(h):
        S = sets[h % D]
        QN, QR, KN, VV = S["QN"], S["QR"], S["KN"], S["VV"]

        def geo(c):
            j, cc = divmod(c, 4)
            r = c - 4 * i
            lo = 128 * r if r > 0 else 0
            return j, cc, r, lo

        def s_mm(c):
            j, cc, r, lo = geo(c)
            sb = SB[c % 3]
            sap = sb.ap[:, lo:TT]
            mm(P, sb, KN[j].ap[:, cc * 128:(cc + 1) * 128], KN[j], QN.ap[:, lo:TT], QN, True, False, out_ap=sap)
            mm(P, sb, k.krb[j].ap[:, cc * 128:(cc + 1) * 128], k.krb[j], QR.ap[:, lo:TT], QR, False, True, out_ap=sap)

        for c0 in range(min(LA, nchunk)):
            s_mm(c0)
        for c in range(nchunk):
            j, cc, r, lo = geo(c)
            if c + LA < nchunk:
                s_mm(c + LA)
            sb = SB[c % 3]
            pt = PT[c % 3]
            sap = sb.ap[:, lo:TT]
            pap = pt.ap[:, lo:TT]
            scap = k.sck_t[:, (j * 4 + cc) * 8 + h:(j * 4 + cc) * 8 + h + 1]
            P.op("act", lambda e, sap=sap, pap=pap, scap=scap: e.activation(pap, sap, AF.Exp, scale=scap),
                 reads=[sb, k.sck[j]], writes=[pt])
            if r >= 0:
                dap = pt.ap[:, lo:lo + 128]
                P.op("dve", lambda e, dap=dap: e.tensor_tensor(dap, dap, k.mask.ap, ALU.mult), reads=[pt, k.mask], writes=[pt])
            mm(P, OB, VV[j].ap[:, cc * 128:(cc + 1) * 128], VV[j], pap, pt, c == 0, c == nchunk - 1, out_ap=OB.ap[:, lo:TT])
            mm(P, LB, k.ones_t[:], k.ones, pap, pt, c == 0, c == nchunk - 1, out_ap=LB.ap[:, lo:TT])

    def fin_a(h):
        P.op("act", lambda e: e.activation(tmp[2].ap, LB.ap, AF.Ln), reads=[LB], writes=[tmp[2]])
        P.op("act", lambda e: e.activation(tmp[2].ap, tmp[2].ap, AF.Exp, scale=-1.0), reads=[tmp[2]], writes=[tmp[2]])
        P.op("dve", lambda e: e.tensor_tensor(tmp[3].ap, OB.ap, tmp[2].ap, ALU.mult), reads=[OB, tmp[2]], writes=[tmp[3]])
        P.op("act", lambda e: e.activation(FSQ.ap, tmp[3].ap, AF.Square), reads=[tmp[3]], writes=[FSQ])

    def fin_b(h):
        mm(P, STB, k.ones_t[:], k.ones, FSQ.ap, FSQ, True, True)
        P.op("act", lambda e: e.activation(tmp[2].ap, STB.ap, AF.Ln, bias=k.eps_t[:, 0:1], scale=1.0 / 128),
             reads=[STB, k.eps], writes=[tmp[2]])
        P.op("act", lambda e: e.activation(tmp[2].ap, tmp[2].ap, AF.Exp, scale=-0.5), reads=[tmp[2]], writes=[tmp[2]])
        P.op("dve", lambda e, h=h: e.scalar_tensor_tensor(out=yt[8 + h].ap, in0=tmp[3].ap, scalar=col[:, C_GOB + h:C_GOB + h + 1],
                                                          in1=tmp[2].ap, op0=ALU.mult, op1=ALU.mult),
             reads=[tmp[3], k.cols, tmp[2]], writes=[yt[8 + h]])

    for h0 in range(D):
        prep(h0)
    for h in range(8):
        scores(h)
        fin_a(h)
        if h + D < 8:
            prep(h + D)
        fin_b(h)
    for dcol in range(DC):
        v, b = k.w4.acquire(k.wo[dcol], (128, DC, 128))
        pb = ps[4 + dcol % 4]
        if not P.dry:
            for cc in range(DC):
                mm(P, pb, v[:, cc, :], b, yt[cc].ap, yt[cc], cc == 0, cc == DC - 1)
        k.w4.release()
        x = k.xt[dcol]
        P.op("dve", lambda e, x=x, pb=pb: e.tensor_tensor(x.ap, x.ap, pb.ap, ALU.add), reads=[x, pb], writes=[x])


def program(k):
    P = k.P
    prologue(k)
    k.w4.prefetch()

    def pages(t):
        return [k.xslots[(c - 4 * t) % NXS] for c in range(DC)]

    def grp(pg, c0):
        s0 = pg[c0].slot
        assert [p_.slot for p_ in pg[c0:c0 + 4]] == [s0, s0 + 1, s0 + 2, s0 + 3]
        return k.xt_t[:, s0:s0 + 4, :]

    k.xt = pages(0)
    for g in range(4):
        P.dma("sp", grp(k.xt, 4 * g), k.x_d[0, :, 4 * g:4 * g + 4, :], writes=k.xt[4 * g:4 * g + 4])
    for t in range(k.n_tiles):
        k.xt = cur = pages(t)
        nxt = pages(t + 1) if t + 1 < k.n_tiles else None
        if "ffn1" in k.stages:
            ffn(k, 0, k.w1g, k.w1u, k.w1d)
        if "mix" in k.stages:
            mixer(k, t)

        def hook(g, t=t, cur=cur, nxt=nxt):
            P.dma("sp", k.o_d[t, :, 4 * g:4 * g + 4, :], grp(cur, 4 * g), reads=cur[4 * g:4 * g + 4])
            if nxt is not None and g < 3:
                P.dma("act", grp(nxt, 4 * g + 4), k.x_d[t + 1, :, 4 * g + 4:4 * g + 8, :], writes=nxt[4 * g + 4:4 * g + 8])

        if nxt is not None:
            P.dma("sp", grp(nxt, 0), k.x_d[t + 1, :, 0:4, :], writes=nxt[0:4])
        if "ffn2" in k.stages:
            ffn(k, 2, k.w2g, k.w2u, k.w2d, hook=hook)
        else:
            for g in range(4):
                hook(g)
    P.wait_bufs("sp", k.xslots)


def build(n_tiles=TILES_PER_CORE, stages=("ffn1", "mix", "ffn2")):
    nc = bass.Bass("TRN2", target_bir_lowering=False)
    P = Prog(nc)
    k = setup(nc, P, n_tiles, stages)
    P.dry = True
    program(k)
    P.dry = False
    k.gu_ctr = 0
    k.dg_ctr = 0
    for r in (k.w4, k.wd):
        r.n_acq = 0
        r.n_loaded = 0
    program(k)
    P.emit()
    return nc, P


def _lay_w_in_out(w):
    Kd, N = w.shape
    return np.ascontiguousarray(w.reshape(Kd // 128, 128, N // 128, 128).transpose(2, 1, 0, 3))


def _lay_wd(w):
    return np.ascontiguousarray(w.reshape(NFC, 128, 4, 512).transpose(2, 1, 0, 3))


def _lay_gain(g):
    return g.reshape(DC, 128).T


def host_inputs(inputs, n_tiles=TILES_PER_CORE, ncores=NCORES, mix=True):
    x = np.asarray(inputs["x"], dtype=np.float32)
    B, S, _ = x.shape
    xt = x.reshape(B * S // TT, TT, DC, 128).transpose(0, 3, 2, 1)
    shared = {
        "w1g": _lay_w_in_out(np.asarray(inputs["ffn1_w_gate"][0])),
        "w1u": _lay_w_in_out(np.asarray(inputs["ffn1_w_up"][0])),
        "w1d": _lay_wd(np.asarray(inputs["ffn1_w_down"][0])),
        "w2g": _lay_w_in_out(np.asarray(inputs["ffn2_w_gate"][0])),
        "w2u": _lay_w_in_out(np.asarray(inputs["ffn2_w_up"][0])),
        "w2d": _lay_wd(np.asarray(inputs["ffn2_w_down"][0])),
        "gains": np.ascontiguousarray(np.concatenate([
            _lay_gain(np.asarray(inputs["ffn1_norm_g"][0])),
            _lay_gain(np.asarray(inputs["mix_norm_g"][0])),
            _lay_gain(np.asarray(inputs["ffn2_norm_g"][0]))], axis=1).astype(np.float32)),
    }
    if mix:
        f32 = np.float32
        w_in = np.asarray(inputs["w_in"][0], dtype=f32)
        partner = (np.arange(64) + 32) % 64
        kr = w_in[:, 3072:3136]
        pad = np.zeros((D, 64), f32)
        w_in_x = np.concatenate([w_in[:, :3072], kr, pad, kr[:, partner], pad], axis=1)
        shared["win"] = _lay_w_in_out(w_in_x)
        wq = np.asarray(inputs["mla_w_q_up"][0], dtype=f32).reshape(512, 8, 192)
        wq_x = np.concatenate([wq, wq[:, :, 128 + partner]], axis=2)
        shared["wq"] = np.ascontiguousarray(wq_x.reshape(4, 128, 8, 256).transpose(2, 1, 0, 3))
        wkv = np.asarray(inputs["mla_w_kv_up"][0], dtype=f32).reshape(512, 8, 256)
        shared["wkv"] = np.ascontiguousarray(wkv.reshape(4, 128, 8, 256).transpose(2, 1, 0, 3))
        shared["wo"] = _lay_w_in_out(np.asarray(inputs["w_out"][0], dtype=f32))
        wk = wkv[:, :, 0:128].reshape(4, 128, 2, 4 * 128)
        shared["wkall"] = np.ascontiguousarray(wk.transpose(2, 1, 0, 3))
        cols = np.zeros((128, NCOL), f32)
        cols[:, C_GQ:C_GQ + 4] = np.asarray(inputs["mla_q_norm_g"][0]).reshape(4, 128).T
        cols[:, C_GKV:C_GKV + 4] = np.asarray(inputs["mla_kv_norm_g"][0]).reshape(4, 128).T
        qg = np.asarray(inputs["mla_q_head_g"][0])
        kg = np.asarray(inputs["mla_k_head_g"][0])
        cols[:, C_GQN] = qg[:128]
        cols[:64, C_GQR] = qg[128:]
        cols[:64, C_GQS] = qg[128 + partner]
        cols[:, C_GKN] = kg[:128]
        cols[:64, C_GKR] = kg[128:]
        cols[:64, C_GKS] = kg[128 + partner]
        cols[:, C_GOA:C_GOA + 8] = np.asarray(inputs["gmlp_out_g"][0]).T
        cols[:, C_GOB:C_GOB + 8] = np.asarray(inputs["mla_out_g"][0]).T
        half = 32
        inv_freq = (1.0 / (10000.0 ** (np.arange(half, dtype=f32) / half))).astype(f32)
        cols[:64, C_IF] = inv_freq[np.arange(64) % 32]
        cols[:64, C_SG] = np.where(np.arange(64) < 32, -1.0, 1.0)
        shared["cols"] = cols
        shared["gvbc"] = np.ascontiguousarray(np.asarray(inputs["gmlp_v_norm_g"][0], dtype=f32).reshape(1, 1024))
        shared["bsbc"] = np.ascontiguousarray(np.asarray(inputs["gmlp_b_s"][0], dtype=f32).reshape(1, 1024))
        ws = np.asarray(inputs["gmlp_w_s"][0], dtype=f32)
        shared["wst"] = np.ascontiguousarray(ws.transpose(2, 0, 1).reshape(128, 1024))
        pos = np.asarray(inputs["positions"]).astype(np.int32).reshape(-1, 1, TT)
    in_maps = []
    for c in range(ncores):
        m = dict(shared)
        m["xT"] = np.ascontiguousarray(xt[c * n_tiles:(c + 1) * n_tiles])
        if mix:
            m["pos"] = np.ascontiguousarray(pos[c * n_tiles:(c + 1) * n_tiles])
        in_maps.append(m)
    return in_maps


def kernel(**inputs):
    nc, P = build()
    in_maps = host_inputs(inputs)
    res = run_bass_kernel_spmd(nc, in_maps, core_ids=list(range(NCORES)))
    outs = [r["oT"] for r in res.results]
    o = np.concatenate(outs, axis=0)
    o = o.transpose(0, 3, 2, 1).reshape(16, 2048, D)
    return np.ascontiguousarray(o.astype(np.float32))
```
